# Optimizing a Trainium2 kernel written in Bass

```python
import math
import jax, jax.numpy as jnp
from jax import lax
import numpy as np

D_MODEL = 1024
BATCH = 4
SEQ = 4096
DEPTH = 2

MIX_WIDTH = D_MODEL
A_WIDTH = MIX_WIDTH // 2
B_WIDTH = MIX_WIDTH - A_WIDTH
A_HEADS = 4
A_V_HEAD = A_WIDTH // A_HEADS
A_QK_HEAD = A_V_HEAD // 2
B_HEAD = 64
B_Q_HEADS = B_WIDTH // B_HEAD
B_KV_HEADS = 2
B_REP = B_Q_HEADS // B_KV_HEADS
GRID_W = 64
ROPE_THETA = 10000.0
NUM_BUCKETS = 32
MAX_DISTANCE = 128
Q_BLOCK = 128
EPS = 1e-6
SPLIT_SIZES = [A_WIDTH, A_WIDTH, A_WIDTH, A_WIDTH,
               B_WIDTH, B_KV_HEADS * B_HEAD, B_KV_HEADS * B_HEAD, B_WIDTH]
IN_COLS = sum(SPLIT_SIZES)
SPLIT_POINTS = [int(v) for v in np.cumsum(SPLIT_SIZES)[:-1]]

kernel_name = "hymba_diffattn_gqa_axialrope_encoder"


def rms_norm(x, g):
    xf = x.astype(jnp.float32)
    y = xf * lax.rsqrt(jnp.mean(xf * xf, axis=-1, keepdims=True) + EPS)
    return (y * g.astype(jnp.float32)).astype(x.dtype)


def t5_bucket(rel):
    nb = NUM_BUCKETS // 2
    max_exact = nb // 2
    n = jnp.abs(rel)
    large = max_exact + (jnp.log(jnp.maximum(n, 1).astype(jnp.float32) / max_exact)
                         / math.log(MAX_DISTANCE / max_exact) * (nb - max_exact)).astype(jnp.int32)
    large = jnp.minimum(large, nb - 1)
    return jnp.where(rel > 0, nb, 0) + jnp.where(n < max_exact, n, large)


def axial_rope_angles(n):
    rows = n // GRID_W
    row = jnp.repeat(jnp.arange(rows, dtype=jnp.float32), GRID_W)
    col = jnp.tile(jnp.arange(GRID_W, dtype=jnp.float32), rows)
    axis_dim = B_HEAD // 2
    inv = ROPE_THETA ** (-jnp.arange(0, axis_dim, 2, dtype=jnp.float32) / axis_dim)
    ang = jnp.concatenate([row[:, None] * inv, col[:, None] * inv], axis=-1)
    return jnp.cos(ang), jnp.sin(ang)


def apply_rope(x, cos, sin):
    xf = x.astype(jnp.float32).reshape(*x.shape[:-1], -1, 2)
    x0, x1 = xf[..., 0], xf[..., 1]
    c = cos[None, :, None, :]
    s = sin[None, :, None, :]
    out = jnp.stack([x0 * c - x1 * s, x0 * s + x1 * c], axis=-1).reshape(x.shape)
    return out.astype(x.dtype)


def diff_attention(q, k, v, lam, rel_bias):
    bn, s_len, h, _, d = q.shape
    dv = v.shape[-1]
    nblk = s_len // Q_BLOCK
    q = q * (d ** -0.5)
    qb = q.reshape(bn, nblk, Q_BLOCK, h, 2, d).transpose(1, 4, 0, 3, 2, 5)
    kt = k.transpose(3, 0, 2, 1, 4)
    vt = v.transpose(0, 2, 1, 3)
    kpos = jnp.arange(s_len, dtype=jnp.int32)
    starts = jnp.arange(nblk, dtype=jnp.int32) * Q_BLOCK

    def block(args):
        qblk, start = args
        qpos = start + jnp.arange(Q_BLOCK, dtype=jnp.int32)
        bias = rel_bias[t5_bucket(kpos[None, :] - qpos[:, None])]
        bias = bias.astype(jnp.float32).transpose(2, 0, 1)
        sc = jnp.einsum('cbhqd,cbhkd->cbhqk', qblk, kt).astype(jnp.float32) + bias
        p = jax.nn.softmax(sc, axis=-1)
        a = p[0] - lam * p[1]
        return jnp.einsum('bhqk,bhkv->bhqv', a.astype(vt.dtype), vt)

    o = lax.map(block, (qb, starts))
    return o.transpose(1, 0, 3, 2, 4).reshape(bn, s_len, h, dv)


def gqa_attention(q, k, v):
    bn, s_len, _, d = q.shape
    nblk = s_len // Q_BLOCK
    q = q * (d ** -0.5)
    qb = q.reshape(bn, nblk, Q_BLOCK, B_KV_HEADS, B_REP, d).transpose(1, 0, 3, 4, 2, 5)
    kt = k.transpose(0, 2, 1, 3)
    vt = v.transpose(0, 2, 1, 3)

    def block(qblk):
        sc = jnp.einsum('bgrqd,bgkd->bgrqk', qblk, kt).astype(jnp.float32)
        p = jax.nn.softmax(sc, axis=-1).astype(vt.dtype)
        return jnp.einsum('bgrqk,bgkd->bgrqd', p, vt)

    o = lax.map(block, qb)
    return o.transpose(1, 0, 4, 2, 3, 5).reshape(bn, s_len, B_Q_HEADS * d)


def setup_inputs(seed: int = 0) -> dict:
    key = jax.random.key(seed)
    ks = jax.random.split(key, 10)
    f32 = jnp.float32
    x = jax.random.normal(ks[0], (BATCH, SEQ, D_MODEL), f32)
    rel_bias = 0.5 * jax.random.normal(ks[1], (NUM_BUCKETS, A_HEADS), f32)
    pre_norm_g = 1.0 + 0.05 * jax.random.normal(ks[2], (DEPTH, D_MODEL), f32)
    w_in = jax.random.normal(ks[3], (DEPTH, D_MODEL, IN_COLS), f32) * D_MODEL ** -0.5
    diff_lambda = 0.1 * jax.random.normal(ks[4], (DEPTH, 4, A_QK_HEAD), f32)
    diff_subln_g = 1.0 + 0.05 * jax.random.normal(ks[5], (DEPTH, A_V_HEAD), f32)
    q_norm_g = 1.0 + 0.05 * jax.random.normal(ks[6], (DEPTH, B_HEAD), f32)
    k_norm_g = 1.0 + 0.05 * jax.random.normal(ks[7], (DEPTH, B_HEAD), f32)
    w_out = jax.random.normal(ks[8], (DEPTH, MIX_WIDTH, D_MODEL), f32) * MIX_WIDTH ** -0.5
    post_norm_g = 1.0 + 0.05 * jax.random.normal(ks[9], (DEPTH, D_MODEL), f32)
    return {"x": x, "rel_bias": rel_bias, "pre_norm_g": pre_norm_g, "w_in": w_in,
            "diff_lambda": diff_lambda, "diff_subln_g": diff_subln_g,
            "q_norm_g": q_norm_g, "k_norm_g": k_norm_g, "w_out": w_out,
            "post_norm_g": post_norm_g}


def reference(x, rel_bias, pre_norm_g, w_in, diff_lambda, diff_subln_g, q_norm_g, k_norm_g,
              w_out, post_norm_g):
    bn, s_len, _ = x.shape
    cos, sin = axial_rope_angles(s_len)
    for l in range(DEPTH):
        h = rms_norm(x, pre_norm_g[l])
        proj = h @ w_in[l]
        aq, ak, av, ag, bq, bk, bv, bg = jnp.split(proj, SPLIT_POINTS, axis=-1)

        lam_init = 0.8 - 0.6 * math.exp(-0.3 * l)
        lp = diff_lambda[l].astype(jnp.float32)
        lam = jnp.exp(jnp.sum(lp[0] * lp[1])) - jnp.exp(jnp.sum(lp[2] * lp[3])) + lam_init
        oa = diff_attention(aq.reshape(bn, s_len, A_HEADS, 2, A_QK_HEAD),
                            ak.reshape(bn, s_len, A_HEADS, 2, A_QK_HEAD),
                            av.reshape(bn, s_len, A_HEADS, A_V_HEAD), lam, rel_bias)
        oa = rms_norm(oa, diff_subln_g[l]) * (1.0 - lam_init)
        ya = oa.reshape(bn, s_len, A_WIDTH) * jax.nn.silu(ag)

        q = apply_rope(rms_norm(bq.reshape(bn, s_len, B_Q_HEADS, B_HEAD), q_norm_g[l]), cos, sin)
        k = apply_rope(rms_norm(bk.reshape(bn, s_len, B_KV_HEADS, B_HEAD), k_norm_g[l]), cos, sin)
        ob = gqa_attention(q, k, bv.reshape(bn, s_len, B_KV_HEADS, B_HEAD))
        yb = ob * jax.nn.silu(bg)

        y = jnp.concatenate([ya, yb], axis=-1) @ w_out[l]
        x = x + rms_norm(y, post_norm_g[l])
    return x
```

```python
import math
from contextlib import ExitStack
import numpy as np
import concourse.bass as bass
import concourse.mybir as mybir
from concourse.bass_utils import run_bass_kernel_spmd

F32 = mybir.dt.float32
BF16 = mybir.dt.bfloat16
AF = mybir.ActivationFunctionType
ALU = mybir.AluOpType
AX = mybir.AxisListType

ENGS = ("pe", "act", "dve", "pool", "sp")
BLOCKNAME = {"pe": "tensor", "act": "scalar", "dve": "vector", "pool": "gpsimd", "sp": "sync"}

FUSED = True
D = 1024
S = 4096
SH = 2048
L = 2
EPS = 1e-6
NKV = 1664
NQG = 2560


class Op:
    __slots__ = ("eng", "emit", "dma", "idx", "deps", "need_inc", "inc_val", "dma_val")


class Prog:
    def __init__(self, nc):
        self.nc = nc
        self.ops = []
        self.last_writer = {}
        self.readers = {}
        self.dma_counts = {}

    def op(self, eng, emit, reads=(), writes=(), dma=None):
        o = Op()
        o.eng, o.emit, o.dma, o.idx = eng, emit, dma, len(self.ops)
        o.need_inc, o.inc_val, o.dma_val = False, 0, 0
        deps = set()
        for r in reads:
            w = self.last_writer.get(r)
            if w is not None:
                deps.add(w)
        for w_ in writes:
            w = self.last_writer.get(w_)
            if w is not None:
                deps.add(w)
            rd = self.readers.get(w_)
            if rd:
                deps.update(rd.values())
        best, final = {}, set()
        for d in deps:
            dop = self.ops[d]
            if dop.dma is not None:
                final.add(d)
            elif dop.eng not in best or best[dop.eng] < d:
                best[dop.eng] = d
        final.update(best.values())
        o.deps = final
        for w_ in writes:
            self.last_writer[w_] = o.idx
            self.readers[w_] = {}
        for r in reads:
            if r in writes:
                continue
            rd = self.readers.setdefault(r, {})
            if dma is not None:
                rd[("dma", o.idx)] = o.idx
            else:
                rd[eng] = o.idx
        if dma is not None:
            self.dma_counts[dma] = self.dma_counts.get(dma, 0) + 16
            o.dma_val = self.dma_counts[dma]
        self.ops.append(o)
        return o

    def emit_all(self, final_wait_keys=()):
        nc, ops = self.nc, self.ops
        for o in ops:
            for d in o.deps:
                dep = ops[d]
                if dep.dma is None and not (dep.eng == "pe" and o.eng == "pe"):
                    dep.need_inc = True
        cnt = {e: 0 for e in ENGS}
        for o in ops:
            if o.dma is None and o.need_inc:
                cnt[o.eng] += 1
                o.inc_val = cnt[o.eng]
        by_eng = {e: [] for e in ENGS}
        for o in ops:
            by_eng[o.eng].append(o)
        with ExitStack() as st:
            engsem = {e: st.enter_context(nc.semaphore("s_" + e)) for e in ENGS}
            dmasem = {k: st.enter_context(nc.semaphore("d%d" % i)) for i, k in enumerate(self.dma_counts)}
            block = st.enter_context(nc.Block())

            def make_body(e):
                def body(eng):
                    known = {}
                    for o in by_eng[e]:
                        waits = {}
                        for d in o.deps:
                            dep = ops[d]
                            if dep.dma is not None:
                                sem, val = dmasem[dep.dma], dep.dma_val
                            else:
                                if dep.eng == "pe" and e == "pe":
                                    continue
                                sem, val = engsem[dep.eng], dep.inc_val
                            if waits.get(sem, 0) < val:
                                waits[sem] = val
                        for sem, val in waits.items():
                            if known.get(sem, 0) < val:
                                eng.wait_ge(sem, val)
                                known[sem] = val
                        ins = o.emit(eng)
                        if o.dma is not None:
                            ins.then_inc(dmasem[o.dma], 16)
                        elif o.need_inc:
                            ins.then_inc(engsem[e], 1)
                    if e == "sp":
                        for k in final_wait_keys:
                            eng.wait_ge(dmasem[k], self.dma_counts[k])
                return body

            for e in ENGS:
                if by_eng[e] or e == "sp":
                    getattr(block, BLOCKNAME[e])(make_body(e))


def t5_bucket_np(rel):
    nb, max_exact = 16, 8
    n = np.abs(rel)
    large = max_exact + (np.log(np.maximum(n, 1).astype(np.float32) / max_exact)
                         / math.log(128 / max_exact) * (nb - max_exact)).astype(np.int32)
    large = np.minimum(large, nb - 1)
    return np.where(rel > 0, nb, 0) + np.where(n < max_exact, n, large)


def near_kind(kc, qb):
    same = (kc < 16) == (qb < 4)
    if same:
        d = kc - 4 * qb
        if -1 <= d <= 4:
            return (d, 0)
        return None
    if (kc, qb) == (16, 3):
        return (4, 1)
    if (kc, qb) == (15, 4):
        return (-1, 1)
    if (kc, qb) == (31, 0):
        return (-1, 2)
    if (kc, qb) == (0, 7):
        return (4, 2)
    return None


def build_program(layers, nqbs, fused, debug=0):
    nc = bass.Bass("TRN2", target_bir_lowering=False)
    NL = len(layers)

    def din(name, shape):
        return nc.dram_tensor(name, shape, F32, kind="ExternalInput")

    xT_t = din("xT", [D, S])
    cos_t, sin_t = din("cosT", [128, S]), din("sinT", [128, S])
    wkv_t, wqg_t, wo_t = din("wkv", [NL, D, NKV]), din("wqg", [NL, D, NQG]), din("wo", [NL, D, D])
    NG = 21 * NL
    gains_t = din("gains", [128, NG])
    dlam_t = din("dlam", [1, NL * 256])
    lconst_t = din("lconst", [1, 2 * NL])
    wtb_t = din("wtb", [128, 4 * 1152])
    cbt_t = din("cbt", [1, 1024])
    cmat_t = din("cmat", [128, 512])
    out_t = nc.dram_tensor("out", [D, SH], F32, kind="ExternalOutput")
    if fused:
        x1_t = nc.dram_tensor("x1s", [D, S], F32)
    hbuf_t = [nc.dram_tensor("hbuf%d" % i, [D, S], BF16) for i in range(NL)]

    dbg = {}
    if debug:
        for nm, w in (("dKA", 4 * S), ("dKB", 2 * S), ("dVA", 32 * 512), ("dVB", 33 * 132), ("dR1", 20480), ("dR2", 20480), ("dR3", 20480)):
            dbg[nm] = nc.dram_tensor(nm, [128, w], BF16, kind="ExternalOutput")
        dbg["dHT"] = nc.dram_tensor("dHT", [128, 8 * 512], BF16, kind="ExternalOutput")
        dbg["dSM"] = nc.dram_tensor("dSM", [128, 64], F32, kind="ExternalOutput")
    P = Prog(nc)
    st = ExitStack()
    with st:
        def sb(name, shape, dt):
            return st.enter_context(nc.sbuf_tensor(name, shape, dt))

        ps = [st.enter_context(nc.psum_tensor("ps%d" % i, [128, 512], F32)) for i in range(8)]
        PK = ["ps%d" % i for i in range(8)]

        KA = sb("KA", [128, 4, S], BF16)
        KB = sb("KB", [128, 2, S], BF16)
        VA = sb("VA", [128, 32, 512], BF16)
        VB = sb("VB", [128, 33, 2, 66], BF16)
        VBF = VB[:].rearrange("p k g d -> p (k g d)")
        WT = sb("WT", [128, 4, 1152], BF16)
        CB = sb("CB", [128, 1024], F32)
        IDB = sb("IDB", [128, 384], BF16)
        ONES2 = sb("ONES2", [128, 128], F32)
        ONESF = sb("ONESF", [128, 128], F32)
        ONESB = sb("ONESB", [128, 128], BF16)
        GN = sb("GN", [128, NG], F32)
        DL = sb("DL", [128, max(512, NL * 256)], F32)
        LC = sb("LC", [128, 2 * NL], F32)
        SM = sb("SM", [128, 64], F32)
        LTMP = sb("LTMP", [128, 128], F32)
        R = sb("R", [128, 20480], BF16)
        WKV = R[:, 0:8 * NKV].rearrange("p (c n) -> p c n", c=8)
        QA = R[:, 0:4096].rearrange("p (h n) -> p h n", h=8)
        QB = R[:, 4096:8192].rearrange("p (h n) -> p h n", h=8)
        GA = R[:, 8192:10240].rearrange("p (h n) -> p h n", h=4)
        GB = R[:, 10240:14336].rearrange("p (h n) -> p h n", h=8)
        HQ = R[:, 14336:18432].rearrange("p (c n) -> p c n", c=8)
        EB = R[:, 18432:20480].rearrange("p (e n) -> p e n", e=4)
        HT = sb("HT", [128, 8, 512], BF16)
        HTa = R[:, 13312:17408].rearrange("p (c n) -> p c n", c=8)
        HTS = [(HT[:], "HT"), (HTa, "HTa")]
        YTb = HT[:].bitcast(F32)
        assert list(YTb.shape) == [128, 8, 256], YTb.shape
        XIN = sb("XIN", [128, 8, 256], F32)
        YT = sb("YT", [128, 8, 256], F32)
        H2 = sb("H2", [128, 8, 256], BF16)
        TMP = [sb("TMP%d" % i, [128, 512], F32) for i in range(6)]
        CSB = sb("CSB", [128, 2, 512], F32)
        WS = [sb("WS%d" % i, [128, 8, 128], BF16) for i in range(4)]
        WOA = [sb("WOA%d" % i, [128, 4, 128], BF16) for i in range(2)]
        WOB = [sb("WOB%d" % i, [128, 8, 128], BF16) for i in range(2)]

        O_PNG, O_POG, O_SUB, O_QN, O_QNS, O_KN, O_KNS = 0, 8 * NL, 16 * NL, 17 * NL, 18 * NL, 19 * NL, 20 * NL
        C_EPS, C_LAM, C_NLAM, C_SUBG, C_S = 0, 1, 1 + NL, 1 + 2 * NL, 1 + 3 * NL

        RKEYS = ["QA", "QB", "GA", "GB", "HQ", "EB0", "EB1", "EB2", "EB3", "HTa"] + ["WKV%d" % j for j in range(13)]
        rot = {}

        def nxt(name, n):
            i = rot.get(name, 0)
            rot[name] = i + 1
            return i % n

        POOLS = {"main": [(TMP[i], "TMP%d" % i) for i in range(6)],
                 "post": [(TMP[i], "TMP%d" % i) for i in range(3)],
                 "jit": [(TMP[i], "TMP%d" % i) for i in range(3, 6)] + [(DL, "DL")]}

        def tmp(pool="main"):
            lst = POOLS[pool]
            return lst[nxt("tmp_" + pool, len(lst))]

        P.op("sp", lambda e: e.dma_start(out=GN[:], in_=gains_t.ap()), writes=["GN"], dma="GN")
        P.op("sp", lambda e: e.dma_start(out=DL[:, 0:NL * 256], in_=bass.AP(dlam_t, 0, [[0, 128], [1, NL * 256]])), writes=["DL"], dma="DL")
        P.op("sp", lambda e: e.dma_start(out=LC[:], in_=bass.AP(lconst_t, 0, [[0, 128], [1, 2 * NL]])), writes=["LC"], dma="LC")
        P.op("sp", lambda e: e.dma_start(out=CB[:], in_=bass.AP(cbt_t, 0, [[0, 128], [1, 1024]])), writes=["CB"], dma="CB")
        P.op("sp", lambda e: e.dma_start(out=ONES2[:], in_=cmat_t.ap()[:, 384:512]), writes=["ONES2"], dma="ONES2")
        P.op("pool", lambda e: e.dma_start(out=IDB[:], in_=cmat_t.ap()[:, 0:384]), writes=["IDB"], dma="IDB")
        P.op("pool", lambda e: e.dma_start(out=WT[:].rearrange("p h n -> p (h n)"), in_=wtb_t.ap()), writes=["WT"], dma="WT")
        P.op("pool", lambda e: e.memset(ONESF[:], 1.0), writes=["ONESF"])
        P.op("pool", lambda e: e.memset(ONESB[:], 1.0), writes=["ONESB"])
        for wi_ in range(2):
            P.op("pool", lambda e, wi_=wi_: e.memset(WOB[wi_][:], 0.0), writes=["WOB%d" % wi_])
        P.op("pool", lambda e: e.memset(SM[:], 0.0), writes=["SM"])
        P.op("pool", lambda e: e.memset(SM[:, C_EPS:C_EPS + 1], EPS), reads=["SM"], writes=["SM"])
        P.op("pool", lambda e: e.memset(VB[:], 0.0), writes=["VB"])
        P.op("pool", lambda e: e.memset(VB[:, 0:32, :, 64:65], 1.0), writes=["VB"])
        EPSB = SM[:, C_EPS:C_EPS + 1]
        for li in range(NL):
            o = li * 256
            P.op("dve", lambda e, o=o: e.tensor_tensor(out=LTMP[:, 0:64], in0=DL[:, o:o + 64], in1=DL[:, o + 64:o + 128], op=ALU.mult),
                 reads=["DL"], writes=["LT0"])
            P.op("dve", lambda e, o=o: e.tensor_tensor(out=LTMP[:, 64:128], in0=DL[:, o + 128:o + 192], in1=DL[:, o + 192:o + 256], op=ALU.mult),
                 reads=["DL"], writes=["LT1"])
            c = C_S + 4 * li
            P.op("dve", lambda e, c=c: e.reduce_sum(out=SM[:, c:c + 1], in_=LTMP[:, 0:64], axis=AX.X), reads=["LT0", "SM"], writes=["SM"])
            P.op("dve", lambda e, c=c: e.reduce_sum(out=SM[:, c + 1:c + 2], in_=LTMP[:, 64:128], axis=AX.X), reads=["LT1", "SM"], writes=["SM"])
            P.op("act", lambda e, c=c: e.activation(out=SM[:, c + 2:c + 4], in_=SM[:, c:c + 2], func=AF.Exp), reads=["SM"], writes=["SM"])
            P.op("dve", lambda e, c=c, li=li: e.tensor_tensor(out=SM[:, C_LAM + li:C_LAM + li + 1], in0=SM[:, c + 2:c + 3],
                                                               in1=SM[:, c + 3:c + 4], op=ALU.subtract), reads=["SM"], writes=["SM"])
            P.op("dve", lambda e, li=li: e.tensor_tensor(out=SM[:, C_LAM + li:C_LAM + li + 1], in0=SM[:, C_LAM + li:C_LAM + li + 1],
                                                         in1=LC[:, li:li + 1], op=ALU.add), reads=["SM", "LC"], writes=["SM"])
            P.op("dve", lambda e, li=li: e.tensor_scalar_mul(out=SM[:, C_NLAM + li:C_NLAM + li + 1], in0=SM[:, C_LAM + li:C_LAM + li + 1],
                                                             scalar1=-1.0), reads=["SM"], writes=["SM"])
            P.op("dve", lambda e, li=li: e.tensor_tensor(out=SM[:, C_SUBG + li:C_SUBG + li + 1], in0=GN[:, O_SUB + li:O_SUB + li + 1],
                                                         in1=LC[:, NL + li:NL + li + 1], op=ALU.mult), reads=["SM", "LC", "GN"], writes=["SM"])

        def rstd_from(ps_ap, pskey, inv_n, n=512, pool="main"):
            t1, k1 = tmp(pool)
            P.op("act", lambda e: e.activation(out=t1[:, 0:n], in_=ps_ap, func=AF.Ln, bias=EPSB, scale=inv_n),
                 reads=["SM"], writes=[pskey, k1])
            P.op("act", lambda e: e.activation(out=t1[:, 0:n], in_=t1[:, 0:n], func=AF.Exp, scale=-0.5), writes=[k1])
            return t1, k1

        def norm_block(src_ap_fn, srckey, gcol0, dst, dstkey, n, dst_slice, pool="main", bank=None):
            b = (nxt("psN", 2) + 6) if bank is None else bank
            for c in range(8):
                t, k = tmp(pool)
                P.op("act", lambda e, c=c, t=t: e.activation(out=t[:, 0:n], in_=src_ap_fn(c), func=AF.Square),
                     reads=[srckey], writes=[k])
                P.op("pe", lambda e, c=c, t=t: e.matmul(ps[b][:, 0:n], lhsT=ONESF[:], rhs=t[:, 0:n], start=(c == 0), stop=(c == 7)),
                     reads=[k, "ONESF"], writes=[PK[b]])
            rs, rk = rstd_from(ps[b][:, 0:n], PK[b], 1.0 / D, n, pool)
            for c in range(8):
                P.op("dve", lambda e, c=c: e.scalar_tensor_tensor(out=dst[:, c, dst_slice], in0=src_ap_fn(c),
                                                                   scalar=GN[:, gcol0 + c:gcol0 + c + 1], in1=rs[:, 0:n],
                                                                   op0=ALU.mult, op1=ALU.mult),
                     reads=[srckey, rk, "GN"], writes=[dstkey])

        def rope_chunk(psA, kA, psB, kB, gcol, gscol, tok0, scale, dst_ap, dstkey, pool="main", ssbank=None):
            sq, ksq = tmp(pool)
            P.op("act", lambda e: e.activation(out=sq[:], in_=psA[:], func=AF.Square), writes=[kA, ksq])
            bss = (nxt("psS", 2) + 6) if ssbank is None else ssbank
            P.op("pe", lambda e: e.matmul(ps[bss][:], lhsT=ONES2[:], rhs=sq[:], start=True, stop=True),
                 reads=[ksq, "ONES2"], writes=[PK[bss]])
            rs, rk = rstd_from(ps[bss][:], PK[bss], 1.0 / 64, 512, pool)
            P.op("sp", lambda e: e.dma_start(out=CSB[:, 0, :], in_=cos_t.ap()[:, tok0:tok0 + 512]), writes=["CS0"], dma="CS0")
            P.op("sp", lambda e: e.dma_start(out=CSB[:, 1, :], in_=sin_t.ap()[:, tok0:tok0 + 512]), writes=["CS1"], dma="CS1")
            t1, k1 = tmp(pool)
            P.op("dve", lambda e: e.scalar_tensor_tensor(out=t1[:], in0=psA[:], scalar=GN[:, gcol:gcol + 1], in1=CSB[:, 0, :],
                                                         op0=ALU.mult, op1=ALU.mult), reads=["GN", "CS0"], writes=[kA, k1])
            t2, k2 = tmp(pool)
            P.op("dve", lambda e: e.scalar_tensor_tensor(out=t2[:], in0=psB[:], scalar=GN[:, gscol:gscol + 1], in1=CSB[:, 1, :],
                                                         op0=ALU.mult, op1=ALU.mult), reads=["GN", "CS1"], writes=[kB, k2])
            P.op("dve", lambda e: e.tensor_tensor(out=t1[:], in0=t1[:], in1=t2[:], op=ALU.add), reads=[k2], writes=[k1])
            for psl, dap in dst_ap:
                P.op("dve", lambda e, psl=psl, dap=dap: e.scalar_tensor_tensor(out=dap, in0=t1[psl, :], scalar=scale, in1=rs[psl, :],
                                                                               op0=ALU.mult, op1=ALU.mult),
                     reads=[k1, rk], writes=[dstkey])

        def load_w_chunk(src_t, li, col0, width=128):
            i = nxt("ws", 4)
            P.op("pool", lambda e: e.dma_start(out=WS[i][:, :, 0:width],
                                               in_=src_t.ap()[li].rearrange("(c p) n -> p c n", p=128)[:, :, col0:col0 + width]),
                 writes=["WS%d" % i], dma="WS%d" % i)
            return WS[i], "WS%d" % i

        def proj_fm(w, wkey, wcol0, rhs, rhskey, bank, m=128, mcol=0):
            for c in range(8):
                P.op("pe", lambda e, c=c: e.matmul(ps[bank][0:m, :], lhsT=w[:, c, wcol0 + mcol:wcol0 + mcol + m], rhs=rhs[:, c, :],
                                                   start=(c == 0), stop=(c == 7)),
                     reads=[wkey, rhskey], writes=[PK[bank]])

        def layer(i):
            li = i
            nqb = nqbs[i]
            first = (i == 0)
            last = (i == NL - 1)
            x_src = xT_t if first else x1_t
            P.op("pool", lambda e: e.memset(SM[:, 60:61], 0.0), writes=RKEYS + ["SM60"])
            for j in range(13):
                P.op("pool", lambda e, j=j: e.dma_start(out=WKV[:, :, j * 128:(j + 1) * 128],
                                                        in_=wkv_t.ap()[li].rearrange("(c p) n -> p c n", p=128)[:, :, j * 128:(j + 1) * 128]),
                     writes=["WKV%d" % j], dma="WKV%d" % j)
            def prep(tb):
                t0 = tb * 512
                HTc, HTk = HTS[tb % 2]
                th = []
                if first:
                    def half(hf):
                        tt = t0 + hf * 256
                        XB, XK = ((XIN, "XIN"), (YT, "YT"))[hf]
                        P.op("sp", lambda e: e.dma_start(out=XB[:], in_=xT_t.ap().rearrange("(c p) t -> p c t", p=128)[:, :, tt:tt + 256]),
                             writes=[XK], dma=XK)
                        norm_block(lambda c: XB[:, c, :], XK, O_PNG + 8 * li, HTc, HTk, 256, slice(hf * 256, hf * 256 + 256))
                        if hf == 1:
                            P.op("sp", lambda e: e.dma_start(out=hbuf_t[i].ap().rearrange("(c p) t -> p c t", p=128)[:, :, t0:t0 + 512], in_=HTc),
                                 reads=[HTk], writes=["hbuf%d_%d" % (i, tb)], dma="HTst_" + HTk)
                    th.append(lambda: half(0))
                    th.append(lambda: half(1))
                else:
                    th.append(lambda: P.op("sp", lambda e: e.dma_start(out=HTc, in_=hbuf_t[i].ap().rearrange("(c p) t -> p c t", p=128)[:, :, t0:t0 + 512]),
                                           reads=["hbuf%d_%d" % (i, tb)], writes=[HTk], dma=HTk))
                return th

            def projs(tb):
                t0 = tb * 512
                HTc, HTk = HTS[tb % 2]
                th = []

                def ka(h):
                    b = nxt("psP", 4)
                    proj_fm(WKV, "WKV%d" % h, h * 128, HTc, HTk, b)
                    if h % 2 == 0:
                        P.op("dve", lambda e: e.tensor_copy(out=KA[:, h, t0:t0 + 512], in_=ps[b][:]), writes=[PK[b], "KA"])
                    else:
                        P.op("act", lambda e: e.activation(out=KA[:, h, t0:t0 + 512], in_=ps[b][:], func=AF.Copy), writes=[PK[b], "KA"])

                def kb(g):
                    bA = nxt("psP", 4)
                    proj_fm(WKV, "WKV%d" % (4 + g), 512 + g * 128, HTc, HTk, bA)
                    bB = nxt("psP", 4)
                    proj_fm(WKV, "WKV%d" % (6 + g), 768 + g * 128, HTc, HTk, bB)
                    rope_chunk(ps[bA], PK[bA], ps[bB], PK[bB], O_KN + li, O_KNS + li, t0, 1.0, [(slice(0, 128), KB[:, g, t0:t0 + 512])], "KB")

                def vv(sub):
                    kc = tb * 4 + sub
                    b = nxt("psV", 2) + 4
                    for c in range(8):
                        P.op("pe", lambda e, c=c: e.matmul(ps[b][:], lhsT=HTc[:, c, sub * 128:(sub + 1) * 128], rhs=WKV[:, c, 1024:1536],
                                                           start=(c == 0), stop=(c == 7)), reads=[HTk, "WKV8", "WKV9", "WKV10", "WKV11"], writes=[PK[b]])
                    P.op("act", lambda e: e.activation(out=VA[:, kc, :], in_=ps[b][:], func=AF.Copy), writes=[PK[b], "VA"])
                    b2 = nxt("psV", 2) + 4
                    for c in range(8):
                        P.op("pe", lambda e, c=c: e.matmul(ps[b2][:, 0:128], lhsT=HTc[:, c, sub * 128:(sub + 1) * 128], rhs=WKV[:, c, 1536:1664],
                                                           start=(c == 0), stop=(c == 7)), reads=[HTk, "WKV12"], writes=[PK[b2]])
                    P.op("dve", lambda e: e.tensor_copy(out=VB[:, kc, :, 0:64], in_=ps[b2][:, 0:128].rearrange("p (g d) -> p g d", g=2)),
                         writes=[PK[b2], "VB"])
                for h in range(4):
                    th.append(lambda h=h: ka(h))
                for g in range(2):
                    th.append(lambda g=g: kb(g))
                for sub in range(4):
                    th.append(lambda sub=sub: vv(sub))
                return th

            def interleave0(a_, b_):
                ia = ib = 0
                while ia < len(a_) or ib < len(b_):
                    if ib >= len(b_) or (ia < len(a_) and ia * len(b_) <= ib * len(a_)):
                        a_[ia]()
                        ia += 1
                    else:
                        b_[ib]()
                        ib += 1
            interleave0(prep(0), [])
            for tb in range(8):
                interleave0(projs(tb), prep(tb + 1) if tb + 1 < 8 else [])
            if debug:
                P.op("sp", lambda e: e.dma_start(out=dbg["dKA"].ap(), in_=KA[:].rearrange("p h n -> p (h n)")), reads=["KA"], dma="dbg")
                P.op("sp", lambda e: e.dma_start(out=dbg["dKB"].ap(), in_=KB[:].rearrange("p h n -> p (h n)")), reads=["KB"], dma="dbg")
                P.op("sp", lambda e: e.dma_start(out=dbg["dVA"].ap(), in_=VA[:].rearrange("p h n -> p (h n)")), reads=["VA"], dma="dbg")
                P.op("sp", lambda e: e.dma_start(out=dbg["dVB"].ap(), in_=VBF), reads=["VB"], dma="dbg")
                P.op("sp", lambda e: e.dma_start(out=dbg["dHT"].ap(), in_=HTa.rearrange("p h n -> p (h n)")), reads=["HTa"], dma="dbg")
                P.op("sp", lambda e: e.dma_start(out=dbg["dSM"].ap(), in_=SM[:]), reads=["SM"], dma="dbg")
            P.op("pool", lambda e: e.memset(SM[:, 61:62], 0.0), writes=RKEYS + ["SM61"])
            P.op("pool", lambda e: e.memset(R[:, 0:8192], 0.0), writes=["QA", "QB"])
            P.op("pool", lambda e: e.memset(GB[64:128, :, :], 0.0), writes=["GB"])

            def jit(qb):
                q0 = qb * 512
                s1, s2 = [], []

                def bq(j):
                    if j == 0:
                        P.op("sp", lambda e: e.dma_start(out=HQ, in_=hbuf_t[i].ap().rearrange("(c p) t -> p c t", p=128)[:, :, q0:q0 + 512]),
                             reads=["hbuf%d_%d" % (i, qb)], writes=["HQ"], dma="HQ")
                    w, wk = load_w_chunk(wqg_t, li, 512 + j * 128)
                    bA = nxt("psJ", 4) + 2
                    proj_fm(w, wk, 0, HQ, "HQ", bA)
                    w2, wk2 = load_w_chunk(wqg_t, li, 1024 + j * 128)
                    bB = nxt("psJ", 4) + 2
                    proj_fm(w2, wk2, 0, HQ, "HQ", bB)
                    rope_chunk(ps[bA], PK[bA], ps[bB], PK[bB], O_QN + li, O_QNS + li, q0, 0.125,
                               [(slice(0, 64), QB[0:64, 2 * j, :]), (slice(64, 128), QB[64:128, 2 * j + 1, :])], "QB",
                               pool="jit", ssbank=6)

                def aq(h):
                    w, wk = load_w_chunk(wqg_t, li, h * 128)
                    b = nxt("psJ", 4) + 2
                    proj_fm(w, wk, 0, HQ, "HQ", b)
                    P.op("dve", lambda e: e.tensor_scalar_mul(out=QA[0:64, 2 * h, :], in0=ps[b][0:64, :], scalar1=0.125),
                         writes=[PK[b], "QA"])
                    P.op("dve", lambda e: e.tensor_scalar_mul(out=QA[64:128, 2 * h + 1, :], in0=ps[b][64:128, :], scalar1=0.125),
                         writes=[PK[b], "QA"])

                def ga(h):
                    w, wk = load_w_chunk(wqg_t, li, 1536 + h * 128)
                    b = nxt("psJ", 4) + 2
                    proj_fm(w, wk, 0, HQ, "HQ", b)
                    P.op("act", lambda e: e.activation(out=GA[:, h, :], in_=ps[b][:], func=AF.Silu), writes=[PK[b], "GA"])

                def gb(j):
                    w, wk = load_w_chunk(wqg_t, li, 2048 + j * 128)
                    for hh in range(2):
                        b = nxt("psJ", 4) + 2
                        proj_fm(w, wk, 0, HQ, "HQ", b, m=64, mcol=hh * 64)
                        P.op("act", lambda e, hh=hh, b=b: e.activation(out=GB[0:64, 2 * j + hh, :], in_=ps[b][0:64, :], func=AF.Silu),
                             writes=[PK[b], "GB"])
                for j in range(4):
                    s1.append(lambda j=j: bq(j))
                    s1.append(lambda j=j: aq(j))
                for h in range(4):
                    s2.append(lambda h=h: ga(h))
                for j in range(4):
                    s2.append(lambda j=j: gb(j))
                return s1, s2

            def post(qb):
                q0 = qb * 512
                s1, s2 = [], []

                def outproj(n):
                    wi = nxt("wo", 2)
                    P.op("pool", lambda e: e.dma_start(
                        out=WOA[wi][:], in_=wo_t.ap()[li][0:512, :].rearrange("(h p) n -> p h n", p=128)[:, :, n * 128:(n + 1) * 128]),
                        writes=["WOA%d" % wi], dma="WOA%d" % wi)
                    P.op("pool", lambda e: e.dma_start(
                        out=WOB[wi][0:64, :, :], in_=wo_t.ap()[li][512:1024, :].rearrange("(h p) n -> p h n", p=64)[:, :, n * 128:(n + 1) * 128]),
                        writes=["WOB%d" % wi], dma="WOB%d" % wi)
                    b = nxt("psO", 2)
                    for h in range(4):
                        P.op("pe", lambda e, h=h: e.matmul(ps[b][:], lhsT=WOA[wi][:, h, :], rhs=GA[:, h, :], start=(h == 0), stop=False),
                             reads=["WOA%d" % wi, "GA"], writes=[PK[b]])
                    for hq in range(8):
                        P.op("pe", lambda e, hq=hq: e.matmul(ps[b][:], lhsT=WOB[wi][:, hq, :], rhs=GB[:, hq, :], start=False, stop=(hq == 7)),
                             reads=["WOB%d" % wi, "GB"], writes=[PK[b]])
                    P.op("dve", lambda e: e.tensor_copy(out=YT[:, n, :], in_=ps[b][:, 0:256]), writes=[PK[b], "YT"])
                    P.op("act", lambda e: e.activation(out=YTb[:, n, :], in_=ps[b][:, 256:512], func=AF.Copy), writes=[PK[b], "HT"])

                def half_a(hf, st):
                    tt = q0 + hf * 256
                    Y, YK = ((YT[:], "YT"), (YTb, "HT"))[hf]
                    P.op("sp", lambda e: e.dma_start(out=XIN[:], in_=x_src.ap().rearrange("(c p) t -> p c t", p=128)[:, :, tt:tt + 256]),
                         reads=(["x1_%d" % qb] if not first else []), writes=["XIN"], dma="XIN")
                    for c in range(8):
                        t, k = tmp("post")
                        P.op("act", lambda e, c=c, t=t: e.activation(out=t[:, 0:256], in_=Y[:, c, :], func=AF.Square), reads=[YK], writes=[k])
                        P.op("pe", lambda e, c=c, t=t: e.matmul(ps[7][:, 0:256], lhsT=ONESF[:], rhs=t[:, 0:256], start=(c == 0), stop=(c == 7)),
                             reads=[k, "ONESF"], writes=[PK[7]])
                    st["rs"] = rstd_from(ps[7][:, 0:256], PK[7], 1.0 / D, 256, "post")

                def half_b(hf, st):
                    tt = q0 + hf * 256
                    Y, YK = ((YT[:], "YT"), (YTb, "HT"))[hf]
                    rs, rk = st["rs"]
                    for c in range(8):
                        P.op("dve", lambda e, c=c: e.scalar_tensor_tensor(out=Y[:, c, :], in0=Y[:, c, :],
                                                                          scalar=GN[:, O_POG + 8 * li + c:O_POG + 8 * li + c + 1],
                                                                          in1=rs[:, 0:256], op0=ALU.mult, op1=ALU.mult),
                             reads=[rk, "GN"], writes=[YK])
                        P.op("dve", lambda e, c=c: e.tensor_tensor(out=Y[:, c, :], in0=Y[:, c, :], in1=XIN[:, c, :], op=ALU.add),
                             reads=["XIN"], writes=[YK])
                    if last:
                        P.op("sp", lambda e: e.dma_start(out=out_t.ap().rearrange("(c p) t -> p c t", p=128)[:, :, tt:tt + 256], in_=Y),
                             reads=[YK], dma="OUT_" + YK)
                    else:
                        P.op("sp", lambda e: e.dma_start(out=x1_t.ap().rearrange("(c p) t -> p c t", p=128)[:, :, tt:tt + 256], in_=Y),
                             reads=[YK], writes=["x1_%d" % qb], dma="X1st_" + YK)

                def half_c(hf):
                    tt = q0 + hf * 256
                    Y, YK = ((YT[:], "YT"), (YTb, "HT"))[hf]
                    norm_block(lambda c: Y[:, c, :], YK, O_PNG + 8 * (li + 1), H2, "H2", 256, slice(0, 256), pool="post", bank=7)
                    P.op("sp", lambda e: e.dma_start(out=hbuf_t[i + 1].ap().rearrange("(c p) t -> p c t", p=128)[:, :, tt:tt + 256], in_=H2[:]),
                         reads=["H2"], writes=["hbuf%d_%d" % (i + 1, qb)], dma="H2st")
                for n in range(8):
                    s1.append(lambda n=n: outproj(n))
                for hf in range(2):
                    st = {}
                    s2.append(lambda hf=hf, st=st: half_a(hf, st))
                    s2.append(lambda hf=hf, st=st: half_b(hf, st))
                    if not last:
                        s2.append(lambda hf=hf: half_c(hf))
                return s1, s2

            def interleave(a, b):
                ia = ib = 0
                while ia < len(a) or ib < len(b):
                    if ib >= len(b) or (ia < len(a) and ia * len(b) <= ib * len(a)):
                        a[ia]()
                        ia += 1
                    else:
                        b[ib]()
                        ib += 1

            def attention(qb):
                tiles = []
                for h in range(4):
                    for kc in range(32):
                        for c in range(2):
                            tiles.append(("A", h, kc, c))
                for hq in range(8):
                    for kc in range(32):
                        tiles.append(("B", hq, kc, 0))
                T = len(tiles)
                sbank = [0] * T
                ebuf = [0] * T
                qk_last = [None] * T
                deferred = []

                def defer(n, fn):
                    deferred.append([n, fn])

                def tick():
                    fire = [d for d in deferred if d[0] <= 0]
                    for d in fire:
                        deferred.remove(d)
                    for d in deferred:
                        d[0] -= 1
                    for d in fire:
                        d[1]()

                def rec_qk(t):
                    kind, hx, kc, c = tiles[t]
                    sbk = nxt("psS3", 3) + 4
                    sbank[t] = sbk
                    if kind == "A":
                        nk = near_kind(kc, qb)
                        pr = slice(c * 64, c * 64 + 64)
                        qk_last[t] = P.op("pe", lambda e: e.matmul(ps[sbk][:], lhsT=KA[:, hx, kc * 128:(kc + 1) * 128], rhs=QA[:, 2 * hx + c, :],
                                                                   start=True, stop=(nk is None)), reads=["KA", "QA"], writes=[PK[sbk]])
                        if nk is not None:
                            d, sel = nk
                            u0 = (4 - d) * 128
                            qk_last[t] = P.op("pe", lambda e: e.matmul(ps[sbk][:], lhsT=IDB[:, sel * 128:(sel + 1) * 128], rhs=WT[:, hx, u0:u0 + 512],
                                                                       start=False, stop=True), reads=["IDB", "WT"], writes=[PK[sbk]])
                    else:
                        g, j, hh = hx // 4, hx // 2, hx % 2
                        pr = slice(hh * 64, hh * 64 + 64)
                        qk_last[t] = P.op("pe", lambda e: e.matmul(ps[sbk][:], lhsT=KB[:, g, kc * 128:(kc + 1) * 128], rhs=QB[:, hx, :],
                                                                   start=True, stop=True), reads=["KB", "QB"], writes=[PK[sbk]])

                def rec_exp(t):
                    kind, hx, kc, c = tiles[t]
                    sbk = sbank[t]
                    ei = nxt("eb", 4)
                    ebuf[t] = ei
                    if kind == "A":
                        ci = (hx * 32 + kc) * 8 + qb
                        eo = P.op("act", lambda e: e.activation(out=EB[:, ei, :], in_=ps[sbk][:], func=AF.Exp, bias=CB[:, ci:ci + 1], scale=1.0),
                                  reads=["CB"], writes=[PK[sbk], "EB%d" % ei])
                    else:
                        eo = P.op("act", lambda e: e.activation(out=EB[:, ei, :], in_=ps[sbk][:], func=AF.Exp), writes=[PK[sbk], "EB%d" % ei])
                    if t % 2 == 0 and t + 1 < T and qk_last[t + 1] is not None:
                        eo.deps.add(qk_last[t + 1].idx)

                def fin_A(h):
                    o0, k0 = tmp()
                    o1, k1 = tmp()
                    l0, kl0 = tmp()
                    l1, kl1 = tmp()
                    P.op("act", lambda e: e.activation(out=l0[:], in_=ps[2][:], func=AF.Ln), writes=[PK[2], kl0])
                    P.op("act", lambda e: e.activation(out=l1[:], in_=ps[3][:], func=AF.Ln), writes=[PK[3], kl1])
                    P.op("dve", lambda e: e.tensor_copy(out=o0[:], in_=ps[0][:]), writes=[PK[0], k0])
                    P.op("dve", lambda e: e.tensor_copy(out=o1[:], in_=ps[1][:]), writes=[PK[1], k1])

                    def s1():
                        P.op("act", lambda e: e.activation(out=l0[:], in_=l0[:], func=AF.Exp, scale=-1.0), writes=[kl0])
                        P.op("act", lambda e: e.activation(out=l1[:], in_=l1[:], func=AF.Exp, scale=-1.0), writes=[kl1])
                        P.op("dve", lambda e: e.tensor_tensor(out=o0[:], in0=o0[:], in1=l0[:], op=ALU.mult), reads=[kl0], writes=[k0])
                        P.op("dve", lambda e: e.tensor_tensor(out=o1[:], in0=o1[:], in1=l1[:], op=ALU.mult), reads=[kl1], writes=[k1])
                        P.op("dve", lambda e: e.scalar_tensor_tensor(out=o0[:], in0=o1[:], scalar=SM[:, C_NLAM + li:C_NLAM + li + 1],
                                                                     in1=o0[:], op0=ALU.mult, op1=ALU.add), reads=[k1, "SM"], writes=[k0])
                        P.op("dve", lambda e: e.tensor_tensor(out=o1[:], in0=o0[:], in1=o0[:], op=ALU.mult), reads=[k0], writes=[k1])

                    def s2():
                        P.op("pe", lambda e: e.matmul(ps[7][:], lhsT=ONESF[:], rhs=o1[:], start=True, stop=True),
                             reads=[k1, "ONESF"], writes=[PK[7]])

                    def s3():
                        P.op("act", lambda e: e.activation(out=l0[:], in_=ps[7][:], func=AF.Ln, bias=EPSB, scale=1.0 / 128),
                             reads=["SM"], writes=[PK[7], kl0])
                        P.op("act", lambda e: e.activation(out=l0[:], in_=l0[:], func=AF.Exp, scale=-0.5), writes=[kl0])

                    def s4():
                        P.op("dve", lambda e: e.scalar_tensor_tensor(out=o0[:], in0=o0[:], scalar=SM[:, C_SUBG + li:C_SUBG + li + 1],
                                                                     in1=l0[:], op0=ALU.mult, op1=ALU.mult), reads=[kl0, "SM"], writes=[k0])
                        P.op("dve", lambda e: e.tensor_tensor(out=GA[:, h, :], in0=o0[:], in1=GA[:, h, :], op=ALU.mult), reads=[k0], writes=["GA"])
                    defer(3, s1)
                    defer(8, s2)
                    defer(12, s3)
                    defer(16, s4)

                def fin_B(hq):
                    ob = hq % 2
                    bb = 2 + (hq % 2)
                    rr, rrk = tmp()
                    rb, rbk = tmp()
                    P.op("dve", lambda e: e.tensor_copy(out=rb[0:64, :], in_=ps[ob][0:64, :]), writes=[PK[ob], rbk])
                    P.op("dve", lambda e: e.reciprocal(out=rr[64:65, :], in_=ps[ob][64:65, :]), writes=[PK[ob], rrk])

                    def s2():
                        P.op("pe", lambda e: e.matmul(ps[bb][0:64, :], lhsT=ONESF[64:65, 0:64], rhs=rr[64:65, :], start=True, stop=True),
                             reads=[rrk, "ONESF"], writes=[PK[bb]])

                    def s3():
                        P.op("dve", lambda e: e.tensor_tensor(out=rb[0:64, :], in0=ps[bb][0:64, :], in1=rb[0:64, :], op=ALU.mult),
                             writes=[PK[bb], rbk])
                        P.op("dve", lambda e: e.tensor_tensor(out=GB[0:64, hq, :], in0=rb[0:64, :], in1=GB[0:64, hq, :], op=ALU.mult),
                             reads=[rbk], writes=["GB"])
                    defer(10, s2)
                    defer(14, s3)

                def rec_pv(t):
                    kind, hx, kc, c = tiles[t]
                    ei = ebuf[t]
                    if kind == "A":
                        P.op("pe", lambda e: e.matmul(ps[c][:], lhsT=VA[:, kc, hx * 128:(hx + 1) * 128], rhs=EB[:, ei, :],
                                                      start=(kc == 0), stop=(kc == 31)), reads=["VA", "EB%d" % ei], writes=[PK[c]])
                        P.op("pe", lambda e: e.matmul(ps[2 + c][:], lhsT=ONESB[:], rhs=EB[:, ei, :],
                                                      start=(kc == 0), stop=(kc == 31)), reads=["ONESB", "EB%d" % ei], writes=[PK[2 + c]])
                        if kc == 31 and c == 1:
                            fin_A(hx)
                    else:
                        g = hx // 4
                        ob = hx % 2
                        off = (kc * 2 + g) * 66
                        P.op("pe", lambda e: e.matmul(ps[ob][:], lhsT=VBF[:, off:off + 128], rhs=EB[:, ei, :],
                                                      start=(kc == 0), stop=(kc == 31)), reads=["VB", "EB%d" % ei], writes=[PK[ob]])
                        if kc == 31:
                            fin_B(hx)

                AHEAD = 2
                for t in range(T + AHEAD):
                    if t < T:
                        rec_qk(t)
                    if t - AHEAD >= 0:
                        rec_exp(t - AHEAD)
                        rec_pv(t - AHEAD)
                    tick()
                while deferred:
                    tick()
                if debug and qb == 0:
                    P.op("sp", lambda e: e.dma_start(out=dbg["dR2"].ap(), in_=R[:]), reads=["QA", "QB", "GA", "GB", "HQ"], dma="dbg")
                    P.op("sp", lambda e: e.dma_start(out=dbg["dR3"].ap(), in_=R[:]), reads=["QA", "QB", "GA", "GB", "HQ"], dma="dbg")
                    P.op("sp", lambda e: e.dma_start(out=dbg["dR1"].ap(), in_=R[:]), reads=["QA", "QB", "GA", "GB", "HQ"], dma="dbg")

            j1, j2 = jit(0)
            interleave(j1 + j2, [])
            for qb in range(nqb):
                attention(qb)
                p1, p2 = post(qb)
                if qb + 1 < nqb:
                    j1, j2 = jit(qb + 1)
                    interleave(p1, j1)
                    interleave(p2, j2)
                else:
                    interleave(p1 + p2, [])

        for i in range(NL):
            layer(i)
        P.emit_all(final_wait_keys=["OUT_YT", "OUT_HT"] + (["dbg"] if debug else []))
    return nc


_PROG_CACHE = {}


def _get_prog(key, layers, nqbs, fused):
    if key not in _PROG_CACHE:
        _PROG_CACHE[key] = build_program(layers, nqbs, fused)
    return _PROG_CACHE[key]


def _local_perm(half):
    own = np.arange(half * SH, half * SH + SH)
    oth = np.arange((1 - half) * SH, (1 - half) * SH + SH)
    return np.concatenate([own, oth])


def _rope_tables(pos):
    inv = (10000.0 ** (-np.arange(0, 32, 2, dtype=np.float32) / np.float32(32))).astype(np.float32)
    row = (pos // 64).astype(np.float32)
    col = (pos % 64).astype(np.float32)
    ang = np.concatenate([row[:, None] * inv, col[:, None] * inv], axis=-1).astype(np.float32)
    c, s = np.cos(ang).astype(np.float32), np.sin(ang).astype(np.float32)
    j = np.arange(64)
    cosT = c[:, j // 2].T
    sgn = np.where(j % 2 == 0, -1.0, 1.0).astype(np.float32)
    sinT = (s[:, j // 2] * sgn[None, :]).T
    return (np.ascontiguousarray(np.concatenate([cosT, cosT], 0), dtype=np.float32),
            np.ascontiguousarray(np.concatenate([sinT, sinT], 0), dtype=np.float32))


def _static_inputs(half, rel_bias):
    perm = _local_perm(half)
    cosT, sinT = _rope_tables(perm)
    p = np.arange(128)[:, None]
    u = np.arange(1152)[None, :]
    bk = t5_bucket_np((p - u + 512).astype(np.int32))
    wtb = np.concatenate([rel_bias[bk, h] for h in range(4)], axis=1).astype(np.float32)
    cbt = np.zeros((4, 32, 8), np.float32)
    for kc in range(32):
        kp = perm[kc * 128:(kc + 1) * 128]
        for qb in range(8):
            qp = perm[qb * 512:(qb + 1) * 512]
            relmin, relmax = kp.min() - qp.max(), kp.max() - qp.min()
            nk = near_kind(kc, qb)
            if relmin >= 91:
                bkt = 31
            elif relmax <= -91:
                bkt = 15
            else:
                bkt = None
                assert nk is not None and (nk[1] == 0 or nk[1] == half + 1), (kc, qb, half)
                assert nk[0] * 128 == kp.min() - qp.min()
            if bkt is not None and nk is not None and (nk[1] == 0 or nk[1] == half + 1):
                bkt = None
                assert nk[0] * 128 == kp.min() - qp.min()
            if bkt is not None:
                cbt[:, kc, qb] = rel_bias[bkt, :]
    eye = np.eye(128, dtype=np.float32)
    z = np.zeros((128, 128), np.float32)
    ones2 = np.zeros((128, 128), np.float32)
    ones2[:64, :64] = 1.0
    ones2[64:, 64:] = 1.0
    cmat = np.concatenate([eye, eye if half == 0 else z, eye if half == 1 else z, ones2], axis=1)
    return dict(cosT=cosT, sinT=sinT, wtb=np.ascontiguousarray(wtb), cbt=np.ascontiguousarray(cbt.reshape(1, 1024)),
                cmat=np.ascontiguousarray(cmat))


def _weight_inputs(ls, pre_norm_g, w_in, diff_lambda, diff_subln_g, q_norm_g, k_norm_g, w_out, post_norm_g):
    nl = len(ls)
    sw = np.arange(512) ^ 1
    sw128 = np.arange(128) ^ 1
    wkv, wqg, wo = [], [], []
    for l in ls:
        w = w_in[l]
        aq, ak, av, ag = w[:, 0:512], w[:, 512:1024], w[:, 1024:1536], w[:, 1536:2048]
        bq, bk, bv, bg = w[:, 2048:2560], w[:, 2560:2688], w[:, 2688:2816], w[:, 2816:3328]
        bks = bk[:, sw128]
        bk2 = np.concatenate([bk[:, 0:64], bk[:, 0:64], bk[:, 64:128], bk[:, 64:128]], 1)
        bk2s = np.concatenate([bks[:, 0:64], bks[:, 0:64], bks[:, 64:128], bks[:, 64:128]], 1)
        wkv.append(np.concatenate([ak, bk2, bk2s, av, bv], 1))
        wqg.append(np.concatenate([aq, bq, bq[:, sw], ag, bg], 1))
        wo.append(w_out[l])
    g = np.zeros((128, 21 * nl), np.float32)
    j64 = np.arange(128) % 64
    for i, l in enumerate(ls):
        g[:, 8 * i:8 * i + 8] = pre_norm_g[l].reshape(8, 128).T
        g[:, 8 * nl + 8 * i:8 * nl + 8 * i + 8] = post_norm_g[l].reshape(8, 128).T
        g[:, 16 * nl + i] = diff_subln_g[l]
        g[:, 17 * nl + i] = q_norm_g[l][j64]
        g[:, 18 * nl + i] = q_norm_g[l][j64 ^ 1]
        g[:, 19 * nl + i] = k_norm_g[l][j64]
        g[:, 20 * nl + i] = k_norm_g[l][j64 ^ 1]
    lam_init = [0.8 - 0.6 * math.exp(-0.3 * l) for l in ls]
    lconst = np.array([lam_init + [1.0 - v for v in lam_init]], np.float32)
    dlam = np.concatenate([diff_lambda[l].reshape(1, 256) for l in ls], 1).astype(np.float32)
    return dict(wkv=np.ascontiguousarray(np.stack(wkv), dtype=np.float32), wqg=np.ascontiguousarray(np.stack(wqg), dtype=np.float32),
                wo=np.ascontiguousarray(np.stack(wo), dtype=np.float32), gains=g, lconst=lconst, dlam=np.ascontiguousarray(dlam))


def kernel(x, rel_bias, pre_norm_g, w_in, diff_lambda, diff_subln_g, q_norm_g, k_norm_g, w_out, post_norm_g):
    x = np.asarray(x, np.float32)
    args = [np.asarray(a, np.float32) for a in (pre_norm_g, w_in, diff_lambda, diff_subln_g, q_norm_g, k_norm_g, w_out, post_norm_g)]
    rel_bias = np.asarray(rel_bias, np.float32)
    stat = [_static_inputs(h, rel_bias) for h in range(2)]
    perms = [_local_perm(h) for h in range(2)]
    if FUSED:
        nc = _get_prog("fused", [0, 1], [8, 4], True)
        wi = _weight_inputs([0, 1], *args)
        in_maps = []
        for c in range(8):
            b, half = c // 2, c % 2
            m = dict(stat[half])
            m.update(wi)
            m["xT"] = np.ascontiguousarray(x[b][perms[half]].T)
            in_maps.append(m)
        res = run_bass_kernel_spmd(nc, in_maps, core_ids=list(range(8)))
        outp = np.empty_like(x)
        for c in range(8):
            b, half = c // 2, c % 2
            outp[b, half * SH:(half + 1) * SH, :] = res.results[c]["out"].T
        return outp
    cur = x
    nc = _get_prog("single", [0], [4], False)
    for l in range(L):
        wi = _weight_inputs([l], *args)
        in_maps = []
        for c in range(8):
            b, half = c // 2, c % 2
            m = dict(stat[half])
            m.update(wi)
            m["xT"] = np.ascontiguousarray(cur[b][perms[half]].T)
            in_maps.append(m)
        res = run_bass_kernel_spmd(nc, in_maps, core_ids=list(range(8)))
        nxt_x = np.empty_like(cur)
        for c in range(8):
            b, half = c // 2, c % 2
            nxt_x[b, half * SH:(half + 1) * SH, :] = res.results[c]["out"].T
        cur = nxt_x
    return cur
```

```python
import math
from contextlib import ExitStack
import numpy as np
import concourse.bass as bass
import concourse.mybir as mybir
from concourse.bass_utils import run_bass_kernel_spmd

F32 = mybir.dt.float32
BF16 = mybir.dt.bfloat16
AF = mybir.ActivationFunctionType
ALU = mybir.AluOpType
AX = mybir.AxisListType

ENGS = ("pe", "act", "dve", "pool", "sp")
BLOCKNAME = {"pe": "tensor", "act": "scalar", "dve": "vector", "pool": "gpsimd", "sp": "sync"}

FUSED = True
D = 1024
S = 4096
SH = 2048
L = 2
EPS = 1e-6
NKV = 1664
NQG = 2560


class Op:
    __slots__ = ("eng", "emit", "dma", "idx", "deps", "need_inc", "inc_val", "dma_val")


class Prog:
    def __init__(self, nc):
        self.nc = nc
        self.ops = []
        self.last_writer = {}
        self.readers = {}
        self.dma_counts = {}

    def op(self, eng, emit, reads=(), writes=(), dma=None):
        o = Op()
        o.eng, o.emit, o.dma, o.idx = eng, emit, dma, len(self.ops)
        o.need_inc, o.inc_val, o.dma_val = False, 0, 0
        deps = set()
        for r in reads:
            w = self.last_writer.get(r)
            if w is not None:
                deps.add(w)
        for w_ in writes:
            w = self.last_writer.get(w_)
            if w is not None:
                deps.add(w)
            rd = self.readers.get(w_)
            if rd:
                deps.update(rd.values())
        best, final = {}, set()
        for d in deps:
            dop = self.ops[d]
            if dop.dma is not None:
                final.add(d)
            elif dop.eng not in best or best[dop.eng] < d:
                best[dop.eng] = d
        final.update(best.values())
        o.deps = final
        for w_ in writes:
            self.last_writer[w_] = o.idx
            self.readers[w_] = {}
        for r in reads:
            if r in writes:
                continue
            rd = self.readers.setdefault(r, {})
            if dma is not None:
                rd[("dma", o.idx)] = o.idx
            else:
                rd[eng] = o.idx
        if dma is not None:
            self.dma_counts[dma] = self.dma_counts.get(dma, 0) + 16
            o.dma_val = self.dma_counts[dma]
        self.ops.append(o)
        return o

    def emit_all(self, final_wait_keys=()):
        nc, ops = self.nc, self.ops
        for o in ops:
            for d in o.deps:
                dep = ops[d]
                if dep.dma is None and not (dep.eng == "pe" and o.eng == "pe"):
                    dep.need_inc = True
        cnt = {e: 0 for e in ENGS}
        for o in ops:
            if o.dma is None and o.need_inc:
                cnt[o.eng] += 1
                o.inc_val = cnt[o.eng]
        by_eng = {e: [] for e in ENGS}
        for o in ops:
            by_eng[o.eng].append(o)
        with ExitStack() as st:
            engsem = {e: st.enter_context(nc.semaphore("s_" + e)) for e in ENGS}
            dmasem = {k: st.enter_context(nc.semaphore("d%d" % i)) for i, k in enumerate(self.dma_counts)}
            block = st.enter_context(nc.Block())

            def make_body(e):
                def body(eng):
                    known = {}
                    for o in by_eng[e]:
                        waits = {}
                        for d in o.deps:
                            dep = ops[d]
                            if dep.dma is not None:
                                sem, val = dmasem[dep.dma], dep.dma_val
                            else:
                                if dep.eng == "pe" and e == "pe":
                                    continue
                                sem, val = engsem[dep.eng], dep.inc_val
                            if waits.get(sem, 0) < val:
                                waits[sem] = val
                        for sem, val in waits.items():
                            if known.get(sem, 0) < val:
                                eng.wait_ge(sem, val)
                                known[sem] = val
                        ins = o.emit(eng)
                        if o.dma is not None:
                            ins.then_inc(dmasem[o.dma], 16)
                        elif o.need_inc:
                            ins.then_inc(engsem[e], 1)
                    if e == "sp":
                        for k in final_wait_keys:
                            eng.wait_ge(dmasem[k], self.dma_counts[k])
                return body

            for e in ENGS:
                if by_eng[e] or e == "sp":
                    getattr(block, BLOCKNAME[e])(make_body(e))


def t5_bucket_np(rel):
    nb, max_exact = 16, 8
    n = np.abs(rel)
    large = max_exact + (np.log(np.maximum(n, 1).astype(np.float32) / max_exact)
                         / math.log(128 / max_exact) * (nb - max_exact)).astype(np.int32)
    large = np.minimum(large, nb - 1)
    return np.where(rel > 0, nb, 0) + np.where(n < max_exact, n, large)


def near_kind(kc, qb):
    same = (kc < 16) == (qb < 4)
    if same:
        d = kc - 4 * qb
        if -1 <= d <= 4:
            return (d, 0)
        return None
    if (kc, qb) == (16, 3):
        return (4, 1)
    if (kc, qb) == (15, 4):
        return (-1, 1)
    if (kc, qb) == (31, 0):
        return (-1, 2)
    if (kc, qb) == (0, 7):
        return (4, 2)
    return None


def build_program(layers, nqbs, fused, debug=0):
    nc = bass.Bass("TRN2", target_bir_lowering=False)
    NL = len(layers)

    def din(name, shape):
        return nc.dram_tensor(name, shape, F32, kind="ExternalInput")

    xT_t = din("xT", [D, S])
    cos_t, sin_t = din("cosT", [128, S]), din("sinT", [128, S])
    wkv_t, wqg_t, wo_t = din("wkv", [NL, D, NKV]), din("wqg", [NL, D, NQG]), din("wo", [NL, D, D])
    NG = 21 * NL
    gains_t = din("gains", [128, NG])
    dlam_t = din("dlam", [1, NL * 256])
    lconst_t = din("lconst", [1, 2 * NL])
    wtb_t = din("wtb", [128, 4 * 1152])
    cbt_t = din("cbt", [1, 1024])
    cmat_t = din("cmat", [128, 512])
    out_t = nc.dram_tensor("out", [D, SH], F32, kind="ExternalOutput")
    if fused:
        x1_t = nc.dram_tensor("x1s", [D, S], F32)
    hbuf_t = [nc.dram_tensor("hbuf%d" % i, [D, S], BF16) for i in range(NL)]

    dbg = {}
    if debug:
        for nm, w in (("dKA", 4 * S), ("dKB", 2 * S), ("dVA", 32 * 512), ("dVB", 33 * 132), ("dR1", 20480), ("dR2", 20480), ("dR3", 20480)):
            dbg[nm] = nc.dram_tensor(nm, [128, w], BF16, kind="ExternalOutput")
        dbg["dHT"] = nc.dram_tensor("dHT", [128, 8 * 512], BF16, kind="ExternalOutput")
        dbg["dSM"] = nc.dram_tensor("dSM", [128, 64], F32, kind="ExternalOutput")
    P = Prog(nc)
    st = ExitStack()
    with st:
        def sb(name, shape, dt):
            return st.enter_context(nc.sbuf_tensor(name, shape, dt))

        ps = [st.enter_context(nc.psum_tensor("ps%d" % i, [128, 512], F32)) for i in range(8)]
        PK = ["ps%d" % i for i in range(8)]

        KA = sb("KA", [128, 4, S], BF16)
        KB = sb("KB", [128, 2, S], BF16)
        VA = sb("VA", [128, 32, 512], BF16)
        VB = sb("VB", [128, 33, 2, 66], BF16)
        VBF = VB[:].rearrange("p k g d -> p (k g d)")
        WT = sb("WT", [128, 4, 1152], BF16)
        CB = sb("CB", [128, 1024], F32)
        IDB = sb("IDB", [128, 384], BF16)
        ONES2 = sb("ONES2", [128, 128], F32)
        ONESF = sb("ONESF", [128, 128], F32)
        ONESB = sb("ONESB", [128, 128], BF16)
        GN = sb("GN", [128, NG], F32)
        DL = sb("DL", [128, max(512, NL * 256)], F32)
        LC = sb("LC", [128, 2 * NL], F32)
        SM = sb("SM", [128, 64], F32)
        LTMP = sb("LTMP", [128, 128], F32)
        R = sb("R", [128, 20480], BF16)
        WKV = R[:, 0:8 * NKV].rearrange("p (c n) -> p c n", c=8)
        QA = R[:, 0:4096].rearrange("p (h n) -> p h n", h=8)
        QB = R[:, 4096:8192].rearrange("p (h n) -> p h n", h=8)
        GA = R[:, 8192:10240].rearrange("p (h n) -> p h n", h=4)
        GB = R[:, 10240:14336].rearrange("p (h n) -> p h n", h=8)
        HQ = R[:, 14336:18432].rearrange("p (c n) -> p c n", c=8)
        EB = R[:, 18432:20480].rearrange("p (e n) -> p e n", e=4)
        HT = sb("HT", [128, 8, 512], BF16)
        HTa = R[:, 13312:17408].rearrange("p (c n) -> p c n", c=8)
        HTS = [(HT[:], "HT"), (HTa, "HTa")]
        YTb = HT[:].bitcast(F32)
        assert list(YTb.shape) == [128, 8, 256], YTb.shape
        XIN = sb("XIN", [128, 8, 256], F32)
        YT = sb("YT", [128, 8, 256], F32)
        H2 = sb("H2", [128, 8, 256], BF16)
        TMP = [sb("TMP%d" % i, [128, 512], F32) for i in range(6)]
        CSB = sb("CSB", [128, 2, 512], F32)
        WS = [sb("WS%d" % i, [128, 8, 128], BF16) for i in range(4)]
        WOA = [sb("WOA%d" % i, [128, 4, 128], BF16) for i in range(2)]
        WOB = [sb("WOB%d" % i, [128, 8, 128], BF16) for i in range(2)]

        O_PNG, O_POG, O_SUB, O_QN, O_QNS, O_KN, O_KNS = 0, 8 * NL, 16 * NL, 17 * NL, 18 * NL, 19 * NL, 20 * NL
        C_EPS, C_LAM, C_NLAM, C_SUBG, C_S = 0, 1, 1 + NL, 1 + 2 * NL, 1 + 3 * NL

        RKEYS = ["QA", "QB", "GA", "GB", "HQ", "EB0", "EB1", "EB2", "EB3", "HTa"] + ["WKV%d" % j for j in range(13)]
        rot = {}

        def nxt(name, n):
            i = rot.get(name, 0)
            rot[name] = i + 1
            return i % n

        POOLS = {"main": [(TMP[i], "TMP%d" % i) for i in range(6)],
                 "post": [(TMP[i], "TMP%d" % i) for i in range(3)],
                 "jit": [(TMP[i], "TMP%d" % i) for i in range(3, 6)] + [(DL, "DL")]}

        def tmp(pool="main"):
            lst = POOLS[pool]
            return lst[nxt("tmp_" + pool, len(lst))]

        P.op("sp", lambda e: e.dma_start(out=GN[:], in_=gains_t.ap()), writes=["GN"], dma="GN")
        P.op("sp", lambda e: e.dma_start(out=DL[:, 0:NL * 256], in_=bass.AP(dlam_t, 0, [[0, 128], [1, NL * 256]])), writes=["DL"], dma="DL")
        P.op("sp", lambda e: e.dma_start(out=LC[:], in_=bass.AP(lconst_t, 0, [[0, 128], [1, 2 * NL]])), writes=["LC"], dma="LC")
        P.op("sp", lambda e: e.dma_start(out=CB[:], in_=bass.AP(cbt_t, 0, [[0, 128], [1, 1024]])), writes=["CB"], dma="CB")
        P.op("sp", lambda e: e.dma_start(out=ONES2[:], in_=cmat_t.ap()[:, 384:512]), writes=["ONES2"], dma="ONES2")
        P.op("pool", lambda e: e.dma_start(out=IDB[:], in_=cmat_t.ap()[:, 0:384]), writes=["IDB"], dma="IDB")
        P.op("pool", lambda e: e.dma_start(out=WT[:].rearrange("p h n -> p (h n)"), in_=wtb_t.ap()), writes=["WT"], dma="WT")
        P.op("pool", lambda e: e.memset(ONESF[:], 1.0), writes=["ONESF"])
        P.op("pool", lambda e: e.memset(ONESB[:], 1.0), writes=["ONESB"])
        for wi_ in range(2):
            P.op("pool", lambda e, wi_=wi_: e.memset(WOB[wi_][:], 0.0), writes=["WOB%d" % wi_])
        P.op("pool", lambda e: e.memset(SM[:], 0.0), writes=["SM"])
        P.op("pool", lambda e: e.memset(SM[:, C_EPS:C_EPS + 1], EPS), reads=["SM"], writes=["SM"])
        P.op("pool", lambda e: e.memset(VB[:], 0.0), writes=["VB"])
        P.op("pool", lambda e: e.memset(VB[:, 0:32, :, 64:65], 1.0), writes=["VB"])
        EPSB = SM[:, C_EPS:C_EPS + 1]
        for li in range(NL):
            o = li * 256
            P.op("dve", lambda e, o=o: e.tensor_tensor(out=LTMP[:, 0:64], in0=DL[:, o:o + 64], in1=DL[:, o + 64:o + 128], op=ALU.mult),
                 reads=["DL"], writes=["LT0"])
            P.op("dve", lambda e, o=o: e.tensor_tensor(out=LTMP[:, 64:128], in0=DL[:, o + 128:o + 192], in1=DL[:, o + 192:o + 256], op=ALU.mult),
                 reads=["DL"], writes=["LT1"])
            c = C_S + 4 * li
            P.op("dve", lambda e, c=c: e.reduce_sum(out=SM[:, c:c + 1], in_=LTMP[:, 0:64], axis=AX.X), reads=["LT0", "SM"], writes=["SM"])
            P.op("dve", lambda e, c=c: e.reduce_sum(out=SM[:, c + 1:c + 2], in_=LTMP[:, 64:128], axis=AX.X), reads=["LT1", "SM"], writes=["SM"])
            P.op("act", lambda e, c=c: e.activation(out=SM[:, c + 2:c + 4], in_=SM[:, c:c + 2], func=AF.Exp), reads=["SM"], writes=["SM"])
            P.op("dve", lambda e, c=c, li=li: e.tensor_tensor(out=SM[:, C_LAM + li:C_LAM + li + 1], in0=SM[:, c + 2:c + 3],
                                                               in1=SM[:, c + 3:c + 4], op=ALU.subtract), reads=["SM"], writes=["SM"])
            P.op("dve", lambda e, li=li: e.tensor_tensor(out=SM[:, C_LAM + li:C_LAM + li + 1], in0=SM[:, C_LAM + li:C_LAM + li + 1],
                                                         in1=LC[:, li:li + 1], op=ALU.add), reads=["SM", "LC"], writes=["SM"])
            P.op("dve", lambda e, li=li: e.tensor_scalar_mul(out=SM[:, C_NLAM + li:C_NLAM + li + 1], in0=SM[:, C_LAM + li:C_LAM + li + 1],
                                                             scalar1=-1.0), reads=["SM"], writes=["SM"])
            P.op("dve", lambda e, li=li: e.tensor_tensor(out=SM[:, C_SUBG + li:C_SUBG + li + 1], in0=GN[:, O_SUB + li:O_SUB + li + 1],
                                                         in1=LC[:, NL + li:NL + li + 1], op=ALU.mult), reads=["SM", "LC", "GN"], writes=["SM"])

        def rstd_from(ps_ap, pskey, inv_n, n=512, pool="main"):
            t1, k1 = tmp(pool)
            P.op("act", lambda e: e.activation(out=t1[:, 0:n], in_=ps_ap, func=AF.Ln, bias=EPSB, scale=inv_n),
                 reads=["SM"], writes=[pskey, k1])
            P.op("act", lambda e: e.activation(out=t1[:, 0:n], in_=t1[:, 0:n], func=AF.Exp, scale=-0.5), writes=[k1])
            return t1, k1

        def norm_block(src_ap_fn, srckey, gcol0, dst, dstkey, n, dst_slice, pool="main", bank=None):
            b = (nxt("psN", 2) + 6) if bank is None else bank
            for c in range(8):
                t, k = tmp(pool)
                P.op("act", lambda e, c=c, t=t: e.activation(out=t[:, 0:n], in_=src_ap_fn(c), func=AF.Square),
                     reads=[srckey], writes=[k])
                P.op("pe", lambda e, c=c, t=t: e.matmul(ps[b][:, 0:n], lhsT=ONESF[:], rhs=t[:, 0:n], start=(c == 0), stop=(c == 7)),
                     reads=[k, "ONESF"], writes=[PK[b]])
            rs, rk = rstd_from(ps[b][:, 0:n], PK[b], 1.0 / D, n, pool)
            for c in range(8):
                P.op("dve", lambda e, c=c: e.scalar_tensor_tensor(out=dst[:, c, dst_slice], in0=src_ap_fn(c),
                                                                   scalar=GN[:, gcol0 + c:gcol0 + c + 1], in1=rs[:, 0:n],
                                                                   op0=ALU.mult, op1=ALU.mult),
                     reads=[srckey, rk, "GN"], writes=[dstkey])

        def rope_chunk(psA, kA, psB, kB, gcol, gscol, tok0, scale, dst_ap, dstkey, pool="main", ssbank=None):
            sq, ksq = tmp(pool)
            P.op("act", lambda e: e.activation(out=sq[:], in_=psA[:], func=AF.Square), writes=[kA, ksq])
            bss = (nxt("psS", 2) + 6) if ssbank is None else ssbank
            P.op("pe", lambda e: e.matmul(ps[bss][:], lhsT=ONES2[:], rhs=sq[:], start=True, stop=True),
                 reads=[ksq, "ONES2"], writes=[PK[bss]])
            rs, rk = rstd_from(ps[bss][:], PK[bss], 1.0 / 64, 512, pool)
            P.op("sp", lambda e: e.dma_start(out=CSB[:, 0, :], in_=cos_t.ap()[:, tok0:tok0 + 512]), writes=["CS0"], dma="CS0")
            P.op("sp", lambda e: e.dma_start(out=CSB[:, 1, :], in_=sin_t.ap()[:, tok0:tok0 + 512]), writes=["CS1"], dma="CS1")
            t1, k1 = tmp(pool)
            P.op("dve", lambda e: e.scalar_tensor_tensor(out=t1[:], in0=psA[:], scalar=GN[:, gcol:gcol + 1], in1=CSB[:, 0, :],
                                                         op0=ALU.mult, op1=ALU.mult), reads=["GN", "CS0"], writes=[kA, k1])
            t2, k2 = tmp(pool)
            P.op("dve", lambda e: e.scalar_tensor_tensor(out=t2[:], in0=psB[:], scalar=GN[:, gscol:gscol + 1], in1=CSB[:, 1, :],
                                                         op0=ALU.mult, op1=ALU.mult), reads=["GN", "CS1"], writes=[kB, k2])
            P.op("dve", lambda e: e.tensor_tensor(out=t1[:], in0=t1[:], in1=t2[:], op=ALU.add), reads=[k2], writes=[k1])
            for psl, dap in dst_ap:
                P.op("dve", lambda e, psl=psl, dap=dap: e.scalar_tensor_tensor(out=dap, in0=t1[psl, :], scalar=scale, in1=rs[psl, :],
                                                                               op0=ALU.mult, op1=ALU.mult),
                     reads=[k1, rk], writes=[dstkey])

        def load_w_chunk(src_t, li, col0, width=128):
            i = nxt("ws", 4)
            P.op("pool", lambda e: e.dma_start(out=WS[i][:, :, 0:width],
                                               in_=src_t.ap()[li].rearrange("(c p) n -> p c n", p=128)[:, :, col0:col0 + width]),
                 writes=["WS%d" % i], dma="WS%d" % i)
            return WS[i], "WS%d" % i

        def proj_fm(w, wkey, wcol0, rhs, rhskey, bank, m=128, mcol=0):
            for c in range(8):
                P.op("pe", lambda e, c=c: e.matmul(ps[bank][0:m, :], lhsT=w[:, c, wcol0 + mcol:wcol0 + mcol + m], rhs=rhs[:, c, :],
                                                   start=(c == 0), stop=(c == 7)),
                     reads=[wkey, rhskey], writes=[PK[bank]])

        def layer(i):
            li = i
            nqb = nqbs[i]
            first = (i == 0)
            last = (i == NL - 1)
            x_src = xT_t if first else x1_t
            P.op("pool", lambda e: e.memset(SM[:, 60:61], 0.0), writes=RKEYS + ["SM60"])
            for j in range(13):
                P.op("pool", lambda e, j=j: e.dma_start(out=WKV[:, :, j * 128:(j + 1) * 128],
                                                        in_=wkv_t.ap()[li].rearrange("(c p) n -> p c n", p=128)[:, :, j * 128:(j + 1) * 128]),
                     writes=["WKV%d" % j], dma="WKV%d" % j)
            def prep(tb):
                t0 = tb * 512
                HTc, HTk = HTS[tb % 2]
                th = []
                if first:
                    def half(hf):
                        tt = t0 + hf * 256
                        XB, XK = ((XIN, "XIN"), (YT, "YT"))[hf]
                        P.op("sp", lambda e: e.dma_start(out=XB[:], in_=xT_t.ap().rearrange("(c p) t -> p c t", p=128)[:, :, tt:tt + 256]),
                             writes=[XK], dma=XK)
                        norm_block(lambda c: XB[:, c, :], XK, O_PNG + 8 * li, HTc, HTk, 256, slice(hf * 256, hf * 256 + 256))
                        if hf == 1:
                            P.op("sp", lambda e: e.dma_start(out=hbuf_t[i].ap().rearrange("(c p) t -> p c t", p=128)[:, :, t0:t0 + 512], in_=HTc),
                                 reads=[HTk], writes=["hbuf%d_%d" % (i, tb)], dma="HTst_" + HTk)
                    th.append(lambda: half(0))
                    th.append(lambda: half(1))
                else:
                    th.append(lambda: P.op("sp", lambda e: e.dma_start(out=HTc, in_=hbuf_t[i].ap().rearrange("(c p) t -> p c t", p=128)[:, :, t0:t0 + 512]),
                                           reads=["hbuf%d_%d" % (i, tb)], writes=[HTk], dma=HTk))
                return th

            def projs(tb):
                t0 = tb * 512
                HTc, HTk = HTS[tb % 2]
                th = []

                def ka(h):
                    b = nxt("psP", 4)
                    proj_fm(WKV, "WKV%d" % h, h * 128, HTc, HTk, b)
                    if h % 2 == 0:
                        P.op("dve", lambda e: e.tensor_copy(out=KA[:, h, t0:t0 + 512], in_=ps[b][:]), writes=[PK[b], "KA"])
                    else:
                        P.op("act", lambda e: e.activation(out=KA[:, h, t0:t0 + 512], in_=ps[b][:], func=AF.Copy), writes=[PK[b], "KA"])

                def kb(g):
                    bA = nxt("psP", 4)
                    proj_fm(WKV, "WKV%d" % (4 + g), 512 + g * 128, HTc, HTk, bA)
                    bB = nxt("psP", 4)
                    proj_fm(WKV, "WKV%d" % (6 + g), 768 + g * 128, HTc, HTk, bB)
                    rope_chunk(ps[bA], PK[bA], ps[bB], PK[bB], O_KN + li, O_KNS + li, t0, 1.0, [(slice(0, 128), KB[:, g, t0:t0 + 512])], "KB")

                def vv(sub):
                    kc = tb * 4 + sub
                    b = nxt("psV", 2) + 4
                    for c in range(8):
                        P.op("pe", lambda e, c=c: e.matmul(ps[b][:], lhsT=HTc[:, c, sub * 128:(sub + 1) * 128], rhs=WKV[:, c, 1024:1536],
                                                           start=(c == 0), stop=(c == 7)), reads=[HTk, "WKV8", "WKV9", "WKV10", "WKV11"], writes=[PK[b]])
                    P.op("act", lambda e: e.activation(out=VA[:, kc, :], in_=ps[b][:], func=AF.Copy), writes=[PK[b], "VA"])
                    b2 = nxt("psV", 2) + 4
                    for c in range(8):
                        P.op("pe", lambda e, c=c: e.matmul(ps[b2][:, 0:128], lhsT=HTc[:, c, sub * 128:(sub + 1) * 128], rhs=WKV[:, c, 1536:1664],
                                                           start=(c == 0), stop=(c == 7)), reads=[HTk, "WKV12"], writes=[PK[b2]])
                    P.op("dve", lambda e: e.tensor_copy(out=VB[:, kc, :, 0:64], in_=ps[b2][:, 0:128].rearrange("p (g d) -> p g d", g=2)),
                         writes=[PK[b2], "VB"])
                for h in range(4):
                    th.append(lambda h=h: ka(h))
                for g in range(2):
                    th.append(lambda g=g: kb(g))
                for sub in range(4):
                    th.append(lambda sub=sub: vv(sub))
                return th

            def interleave0(a_, b_):
                ia = ib = 0
                while ia < len(a_) or ib < len(b_):
                    if ib >= len(b_) or (ia < len(a_) and ia * len(b_) <= ib * len(a_)):
                        a_[ia]()
                        ia += 1
                    else:
                        b_[ib]()
                        ib += 1
            interleave0(prep(0), [])
            for tb in range(8):
                interleave0(projs(tb), prep(tb + 1) if tb + 1 < 8 else [])
            if debug:
                P.op("sp", lambda e: e.dma_start(out=dbg["dKA"].ap(), in_=KA[:].rearrange("p h n -> p (h n)")), reads=["KA"], dma="dbg")
                P.op("sp", lambda e: e.dma_start(out=dbg["dKB"].ap(), in_=KB[:].rearrange("p h n -> p (h n)")), reads=["KB"], dma="dbg")
                P.op("sp", lambda e: e.dma_start(out=dbg["dVA"].ap(), in_=VA[:].rearrange("p h n -> p (h n)")), reads=["VA"], dma="dbg")
                P.op("sp", lambda e: e.dma_start(out=dbg["dVB"].ap(), in_=VBF), reads=["VB"], dma="dbg")
                P.op("sp", lambda e: e.dma_start(out=dbg["dHT"].ap(), in_=HTa.rearrange("p h n -> p (h n)")), reads=["HTa"], dma="dbg")
                P.op("sp", lambda e: e.dma_start(out=dbg["dSM"].ap(), in_=SM[:]), reads=["SM"], dma="dbg")
            P.op("pool", lambda e: e.memset(SM[:, 61:62], 0.0), writes=RKEYS + ["SM61"])
            P.op("pool", lambda e: e.memset(R[:, 0:8192], 0.0), writes=["QA", "QB"])
            P.op("pool", lambda e: e.memset(GB[64:128, :, :], 0.0), writes=["GB"])

            def jit(qb):
                q0 = qb * 512
                s1, s2 = [], []

                def bq(j):
                    if j == 0:
                        P.op("sp", lambda e: e.dma_start(out=HQ, in_=hbuf_t[i].ap().rearrange("(c p) t -> p c t", p=128)[:, :, q0:q0 + 512]),
                             reads=["hbuf%d_%d" % (i, qb)], writes=["HQ"], dma="HQ")
                    w, wk = load_w_chunk(wqg_t, li, 512 + j * 128)
                    bA = nxt("psJ", 4) + 2
                    proj_fm(w, wk, 0, HQ, "HQ", bA)
                    w2, wk2 = load_w_chunk(wqg_t, li, 1024 + j * 128)
                    bB = nxt("psJ", 4) + 2
                    proj_fm(w2, wk2, 0, HQ, "HQ", bB)
                    rope_chunk(ps[bA], PK[bA], ps[bB], PK[bB], O_QN + li, O_QNS + li, q0, 0.125,
                               [(slice(0, 64), QB[0:64, 2 * j, :]), (slice(64, 128), QB[64:128, 2 * j + 1, :])], "QB",
                               pool="jit", ssbank=6)

                def aq(h):
                    w, wk = load_w_chunk(wqg_t, li, h * 128)
                    b = nxt("psJ", 4) + 2
                    proj_fm(w, wk, 0, HQ, "HQ", b)
                    P.op("dve", lambda e: e.tensor_scalar_mul(out=QA[0:64, 2 * h, :], in0=ps[b][0:64, :], scalar1=0.125),
                         writes=[PK[b], "QA"])
                    P.op("dve", lambda e: e.tensor_scalar_mul(out=QA[64:128, 2 * h + 1, :], in0=ps[b][64:128, :], scalar1=0.125),
                         writes=[PK[b], "QA"])

                def ga(h):
                    w, wk = load_w_chunk(wqg_t, li, 1536 + h * 128)
                    b = nxt("psJ", 4) + 2
                    proj_fm(w, wk, 0, HQ, "HQ", b)
                    P.op("act", lambda e: e.activation(out=GA[:, h, :], in_=ps[b][:], func=AF.Silu), writes=[PK[b], "GA"])

                def gb(j):
                    w, wk = load_w_chunk(wqg_t, li, 2048 + j * 128)
                    for hh in range(2):
                        b = nxt("psJ", 4) + 2
                        proj_fm(w, wk, 0, HQ, "HQ", b, m=64, mcol=hh * 64)
                        P.op("act", lambda e, hh=hh, b=b: e.activation(out=GB[0:64, 2 * j + hh, :], in_=ps[b][0:64, :], func=AF.Silu),
                             writes=[PK[b], "GB"])
                for j in range(4):
                    s1.append(lambda j=j: bq(j))
                    s1.append(lambda j=j: aq(j))
                for h in range(4):
                    s2.append(lambda h=h: ga(h))
                for j in range(4):
                    s2.append(lambda j=j: gb(j))
                return s1, s2

            def post(qb):
                q0 = qb * 512
                s1, s2 = [], []

                def outproj(n):
                    wi = nxt("wo", 2)
                    P.op("pool", lambda e: e.dma_start(
                        out=WOA[wi][:], in_=wo_t.ap()[li][0:512, :].rearrange("(h p) n -> p h n", p=128)[:, :, n * 128:(n + 1) * 128]),
                        writes=["WOA%d" % wi], dma="WOA%d" % wi)
                    P.op("pool", lambda e: e.dma_start(
                        out=WOB[wi][0:64, :, :], in_=wo_t.ap()[li][512:1024, :].rearrange("(h p) n -> p h n", p=64)[:, :, n * 128:(n + 1) * 128]),
                        writes=["WOB%d" % wi], dma="WOB%d" % wi)
                    b = nxt("psO", 2)
                    for h in range(4):
                        P.op("pe", lambda e, h=h: e.matmul(ps[b][:], lhsT=WOA[wi][:, h, :], rhs=GA[:, h, :], start=(h == 0), stop=False),
                             reads=["WOA%d" % wi, "GA"], writes=[PK[b]])
                    for hq in range(8):
                        P.op("pe", lambda e, hq=hq: e.matmul(ps[b][:], lhsT=WOB[wi][:, hq, :], rhs=GB[:, hq, :], start=False, stop=(hq == 7)),
                             reads=["WOB%d" % wi, "GB"], writes=[PK[b]])
                    P.op("dve", lambda e: e.tensor_copy(out=YT[:, n, :], in_=ps[b][:, 0:256]), writes=[PK[b], "YT"])
                    P.op("act", lambda e: e.activation(out=YTb[:, n, :], in_=ps[b][:, 256:512], func=AF.Copy), writes=[PK[b], "HT"])

                def half_a(hf, st):
                    tt = q0 + hf * 256
                    Y, YK = ((YT[:], "YT"), (YTb, "HT"))[hf]
                    P.op("sp", lambda e: e.dma_start(out=XIN[:], in_=x_src.ap().rearrange("(c p) t -> p c t", p=128)[:, :, tt:tt + 256]),
                         reads=(["x1_%d" % qb] if not first else []), writes=["XIN"], dma="XIN")
                    for c in range(8):
                        t, k = tmp("post")
                        P.op("act", lambda e, c=c, t=t: e.activation(out=t[:, 0:256], in_=Y[:, c, :], func=AF.Square), reads=[YK], writes=[k])
                        P.op("pe", lambda e, c=c, t=t: e.matmul(ps[7][:, 0:256], lhsT=ONESF[:], rhs=t[:, 0:256], start=(c == 0), stop=(c == 7)),
                             reads=[k, "ONESF"], writes=[PK[7]])
                    st["rs"] = rstd_from(ps[7][:, 0:256], PK[7], 1.0 / D, 256, "post")

                def half_b(hf, st):
                    tt = q0 + hf * 256
                    Y, YK = ((YT[:], "YT"), (YTb, "HT"))[hf]
                    rs, rk = st["rs"]
                    for c in range(8):
                        P.op("dve", lambda e, c=c: e.scalar_tensor_tensor(out=Y[:, c, :], in0=Y[:, c, :],
                                                                          scalar=GN[:, O_POG + 8 * li + c:O_POG + 8 * li + c + 1],
                                                                          in1=rs[:, 0:256], op0=ALU.mult, op1=ALU.mult),
                             reads=[rk, "GN"], writes=[YK])
                        P.op("dve", lambda e, c=c: e.tensor_tensor(out=Y[:, c, :], in0=Y[:, c, :], in1=XIN[:, c, :], op=ALU.add),
                             reads=["XIN"], writes=[YK])
                    if last:
                        P.op("sp", lambda e: e.dma_start(out=out_t.ap().rearrange("(c p) t -> p c t", p=128)[:, :, tt:tt + 256], in_=Y),
                             reads=[YK], dma="OUT_" + YK)
                    else:
                        P.op("sp", lambda e: e.dma_start(out=x1_t.ap().rearrange("(c p) t -> p c t", p=128)[:, :, tt:tt + 256], in_=Y),
                             reads=[YK], writes=["x1_%d" % qb], dma="X1st_" + YK)

                def half_c(hf):
                    tt = q0 + hf * 256
                    Y, YK = ((YT[:], "YT"), (YTb, "HT"))[hf]
                    norm_block(lambda c: Y[:, c, :], YK, O_PNG + 8 * (li + 1), H2, "H2", 256, slice(0, 256), pool="post", bank=7)
                    P.op("sp", lambda e: e.dma_start(out=hbuf_t[i + 1].ap().rearrange("(c p) t -> p c t", p=128)[:, :, tt:tt + 256], in_=H2[:]),
                         reads=["H2"], writes=["hbuf%d_%d" % (i + 1, qb)], dma="H2st")
                for n in range(8):
                    s1.append(lambda n=n: outproj(n))
                for hf in range(2):
                    st = {}
                    s2.append(lambda hf=hf, st=st: half_a(hf, st))
                    s2.append(lambda hf=hf, st=st: half_b(hf, st))
                    if not last:
                        s2.append(lambda hf=hf: half_c(hf))
                return s1, s2

            def interleave(a, b):
                ia = ib = 0
                while ia < len(a) or ib < len(b):
                    if ib >= len(b) or (ia < len(a) and ia * len(b) <= ib * len(a)):
                        a[ia]()
                        ia += 1
                    else:
                        b[ib]()
                        ib += 1

            def attention(qb):
                tiles = []
                for h in range(4):
                    for kc in range(32):
                        for c in range(2):
                            tiles.append(("A", h, kc, c))
                for hq in range(8):
                    for kc in range(32):
                        tiles.append(("B", hq, kc, 0))
                T = len(tiles)
                sbank = [0] * T
                ebuf = [0] * T
                qk_last = [None] * T
                deferred = []

                def defer(n, fn):
                    deferred.append([n, fn])

                def tick():
                    fire = [d for d in deferred if d[0] <= 0]
                    for d in fire:
                        deferred.remove(d)
                    for d in deferred:
                        d[0] -= 1
                    for d in fire:
                        d[1]()

                def rec_qk(t):
                    kind, hx, kc, c = tiles[t]
                    sbk = nxt("psS3", 3) + 4
                    sbank[t] = sbk
                    if kind == "A":
                        nk = near_kind(kc, qb)
                        pr = slice(c * 64, c * 64 + 64)
                        qk_last[t] = P.op("pe", lambda e: e.matmul(ps[sbk][:], lhsT=KA[:, hx, kc * 128:(kc + 1) * 128], rhs=QA[:, 2 * hx + c, :],
                                                                   start=True, stop=(nk is None)), reads=["KA", "QA"], writes=[PK[sbk]])
                        if nk is not None:
                            d, sel = nk
                            u0 = (4 - d) * 128
                            qk_last[t] = P.op("pe", lambda e: e.matmul(ps[sbk][:], lhsT=IDB[:, sel * 128:(sel + 1) * 128], rhs=WT[:, hx, u0:u0 + 512],
                                                                       start=False, stop=True), reads=["IDB", "WT"], writes=[PK[sbk]])
                    else:
                        g, j, hh = hx // 4, hx // 2, hx % 2
                        pr = slice(hh * 64, hh * 64 + 64)
                        qk_last[t] = P.op("pe", lambda e: e.matmul(ps[sbk][:], lhsT=KB[:, g, kc * 128:(kc + 1) * 128], rhs=QB[:, hx, :],
                                                                   start=True, stop=True), reads=["KB", "QB"], writes=[PK[sbk]])

                def rec_exp(t):
                    kind, hx, kc, c = tiles[t]
                    sbk = sbank[t]
                    ei = nxt("eb", 4)
                    ebuf[t] = ei
                    if kind == "A":
                        ci = (hx * 32 + kc) * 8 + qb
                        eo = P.op("act", lambda e: e.activation(out=EB[:, ei, :], in_=ps[sbk][:], func=AF.Exp, bias=CB[:, ci:ci + 1], scale=1.0),
                                  reads=["CB"], writes=[PK[sbk], "EB%d" % ei])
                    else:
                        eo = P.op("act", lambda e: e.activation(out=EB[:, ei, :], in_=ps[sbk][:], func=AF.Exp), writes=[PK[sbk], "EB%d" % ei])
                    if kind == "B" and kc % 2 == 0 and t + 1 < T and tiles[t + 1][0] == "B" and qk_last[t + 1] is not None:
                        eo.deps.add(qk_last[t + 1].idx)

                def fin_A(h):
                    o0, k0 = tmp()
                    o1, k1 = tmp()
                    l0, kl0 = tmp()
                    l1, kl1 = tmp()
                    P.op("act", lambda e: e.activation(out=l0[:], in_=ps[2][:], func=AF.Ln), writes=[PK[2], kl0])
                    P.op("act", lambda e: e.activation(out=l1[:], in_=ps[3][:], func=AF.Ln), writes=[PK[3], kl1])
                    P.op("dve", lambda e: e.tensor_copy(out=o0[:], in_=ps[0][:]), writes=[PK[0], k0])
                    P.op("dve", lambda e: e.tensor_copy(out=o1[:], in_=ps[1][:]), writes=[PK[1], k1])

                    def s1():
                        P.op("act", lambda e: e.activation(out=l0[:], in_=l0[:], func=AF.Exp, scale=-1.0), writes=[kl0])
                        P.op("act", lambda e: e.activation(out=l1[:], in_=l1[:], func=AF.Exp, scale=-1.0), writes=[kl1])
                        P.op("dve", lambda e: e.tensor_tensor(out=o0[:], in0=o0[:], in1=l0[:], op=ALU.mult), reads=[kl0], writes=[k0])
                        P.op("dve", lambda e: e.tensor_tensor(out=o1[:], in0=o1[:], in1=l1[:], op=ALU.mult), reads=[kl1], writes=[k1])
                        P.op("dve", lambda e: e.scalar_tensor_tensor(out=o0[:], in0=o1[:], scalar=SM[:, C_NLAM + li:C_NLAM + li + 1],
                                                                     in1=o0[:], op0=ALU.mult, op1=ALU.add), reads=[k1, "SM"], writes=[k0])
                        P.op("dve", lambda e: e.tensor_tensor(out=o1[:], in0=o0[:], in1=o0[:], op=ALU.mult), reads=[k0], writes=[k1])

                    def s2():
                        P.op("pe", lambda e: e.matmul(ps[7][:], lhsT=ONESF[:], rhs=o1[:], start=True, stop=True),
                             reads=[k1, "ONESF"], writes=[PK[7]])

                    def s3():
                        P.op("act", lambda e: e.activation(out=l0[:], in_=ps[7][:], func=AF.Ln, bias=EPSB, scale=1.0 / 128),
                             reads=["SM"], writes=[PK[7], kl0])
                        P.op("act", lambda e: e.activation(out=l0[:], in_=l0[:], func=AF.Exp, scale=-0.5), writes=[kl0])

                    def s4():
                        P.op("dve", lambda e: e.scalar_tensor_tensor(out=o0[:], in0=o0[:], scalar=SM[:, C_SUBG + li:C_SUBG + li + 1],
                                                                     in1=l0[:], op0=ALU.mult, op1=ALU.mult), reads=[kl0, "SM"], writes=[k0])
                        P.op("dve", lambda e: e.tensor_tensor(out=GA[:, h, :], in0=o0[:], in1=GA[:, h, :], op=ALU.mult), reads=[k0], writes=["GA"])
                    defer(3, s1)
                    defer(8, s2)
                    defer(12, s3)
                    defer(16, s4)

                def fin_B(hq):
                    ob = hq % 2
                    bb = 2 + (hq % 2)
                    rr, rrk = tmp()
                    rb, rbk = tmp()
                    P.op("dve", lambda e: e.tensor_copy(out=rb[0:64, :], in_=ps[ob][0:64, :]), writes=[PK[ob], rbk])
                    P.op("dve", lambda e: e.reciprocal(out=rr[64:65, :], in_=ps[ob][64:65, :]), writes=[PK[ob], rrk])

                    def s2():
                        P.op("pe", lambda e: e.matmul(ps[bb][0:64, :], lhsT=ONESF[64:65, 0:64], rhs=rr[64:65, :], start=True, stop=True),
                             reads=[rrk, "ONESF"], writes=[PK[bb]])

                    def s3():
                        P.op("dve", lambda e: e.tensor_tensor(out=rb[0:64, :], in0=ps[bb][0:64, :], in1=rb[0:64, :], op=ALU.mult),
                             writes=[PK[bb], rbk])
                        P.op("dve", lambda e: e.tensor_tensor(out=GB[0:64, hq, :], in0=rb[0:64, :], in1=GB[0:64, hq, :], op=ALU.mult),
                             reads=[rbk], writes=["GB"])
                    defer(10, s2)
                    defer(14, s3)

                def rec_pv(t):
                    kind, hx, kc, c = tiles[t]
                    ei = ebuf[t]
                    if kind == "A":
                        P.op("pe", lambda e: e.matmul(ps[c][:], lhsT=VA[:, kc, hx * 128:(hx + 1) * 128], rhs=EB[:, ei, :],
                                                      start=(kc == 0), stop=(kc == 31)), reads=["VA", "EB%d" % ei], writes=[PK[c]])
                        P.op("pe", lambda e: e.matmul(ps[2 + c][:], lhsT=ONESB[:], rhs=EB[:, ei, :],
                                                      start=(kc == 0), stop=(kc == 31)), reads=["ONESB", "EB%d" % ei], writes=[PK[2 + c]])
                        if kc == 31 and c == 1:
                            fin_A(hx)
                    else:
                        g = hx // 4
                        ob = hx % 2
                        off = (kc * 2 + g) * 66
                        P.op("pe", lambda e: e.matmul(ps[ob][:], lhsT=VBF[:, off:off + 128], rhs=EB[:, ei, :],
                                                      start=(kc == 0), stop=(kc == 31)), reads=["VB", "EB%d" % ei], writes=[PK[ob]])
                        if kc == 31:
                            fin_B(hx)

                AHEAD = 2
                for t in range(T + AHEAD):
                    if t < T:
                        rec_qk(t)
                    if t - AHEAD >= 0:
                        rec_exp(t - AHEAD)
                        rec_pv(t - AHEAD)
                    tick()
                while deferred:
                    tick()
                if debug and qb == 0:
                    P.op("sp", lambda e: e.dma_start(out=dbg["dR2"].ap(), in_=R[:]), reads=["QA", "QB", "GA", "GB", "HQ"], dma="dbg")
                    P.op("sp", lambda e: e.dma_start(out=dbg["dR3"].ap(), in_=R[:]), reads=["QA", "QB", "GA", "GB", "HQ"], dma="dbg")
                    P.op("sp", lambda e: e.dma_start(out=dbg["dR1"].ap(), in_=R[:]), reads=["QA", "QB", "GA", "GB", "HQ"], dma="dbg")

            j1, j2 = jit(0)
            interleave(j1 + j2, [])
            for qb in range(nqb):
                attention(qb)
                p1, p2 = post(qb)
                if qb + 1 < nqb:
                    j1, j2 = jit(qb + 1)
                    interleave(p1, j1)
                    interleave(p2, j2)
                else:
                    interleave(p1 + p2, [])

        for i in range(NL):
            layer(i)
        P.emit_all(final_wait_keys=["OUT_YT", "OUT_HT"] + (["dbg"] if debug else []))
    return nc


_PROG_CACHE = {}


def _get_prog(key, layers, nqbs, fused):
    if key not in _PROG_CACHE:
        _PROG_CACHE[key] = build_program(layers, nqbs, fused)
    return _PROG_CACHE[key]


def _local_perm(half):
    own = np.arange(half * SH, half * SH + SH)
    oth = np.arange((1 - half) * SH, (1 - half) * SH + SH)
    return np.concatenate([own, oth])


def _rope_tables(pos):
    inv = (10000.0 ** (-np.arange(0, 32, 2, dtype=np.float32) / np.float32(32))).astype(np.float32)
    row = (pos // 64).astype(np.float32)
    col = (pos % 64).astype(np.float32)
    ang = np.concatenate([row[:, None] * inv, col[:, None] * inv], axis=-1).astype(np.float32)
    c, s = np.cos(ang).astype(np.float32), np.sin(ang).astype(np.float32)
    j = np.arange(64)
    cosT = c[:, j // 2].T
    sgn = np.where(j % 2 == 0, -1.0, 1.0).astype(np.float32)
    sinT = (s[:, j // 2] * sgn[None, :]).T
    return (np.ascontiguousarray(np.concatenate([cosT, cosT], 0), dtype=np.float32),
            np.ascontiguousarray(np.concatenate([sinT, sinT], 0), dtype=np.float32))


def _static_inputs(half, rel_bias):
    perm = _local_perm(half)
    cosT, sinT = _rope_tables(perm)
    p = np.arange(128)[:, None]
    u = np.arange(1152)[None, :]
    bk = t5_bucket_np((p - u + 512).astype(np.int32))
    wtb = np.concatenate([rel_bias[bk, h] for h in range(4)], axis=1).astype(np.float32)
    cbt = np.zeros((4, 32, 8), np.float32)
    for kc in range(32):
        kp = perm[kc * 128:(kc + 1) * 128]
        for qb in range(8):
            qp = perm[qb * 512:(qb + 1) * 512]
            relmin, relmax = kp.min() - qp.max(), kp.max() - qp.min()
            nk = near_kind(kc, qb)
            if relmin >= 91:
                bkt = 31
            elif relmax <= -91:
                bkt = 15
            else:
                bkt = None
                assert nk is not None and (nk[1] == 0 or nk[1] == half + 1), (kc, qb, half)
                assert nk[0] * 128 == kp.min() - qp.min()
            if bkt is not None and nk is not None and (nk[1] == 0 or nk[1] == half + 1):
                bkt = None
                assert nk[0] * 128 == kp.min() - qp.min()
            if bkt is not None:
                cbt[:, kc, qb] = rel_bias[bkt, :]
    eye = np.eye(128, dtype=np.float32)
    z = np.zeros((128, 128), np.float32)
    ones2 = np.zeros((128, 128), np.float32)
    ones2[:64, :64] = 1.0
    ones2[64:, 64:] = 1.0
    cmat = np.concatenate([eye, eye if half == 0 else z, eye if half == 1 else z, ones2], axis=1)
    return dict(cosT=cosT, sinT=sinT, wtb=np.ascontiguousarray(wtb), cbt=np.ascontiguousarray(cbt.reshape(1, 1024)),
                cmat=np.ascontiguousarray(cmat))


def _weight_inputs(ls, pre_norm_g, w_in, diff_lambda, diff_subln_g, q_norm_g, k_norm_g, w_out, post_norm_g):
    nl = len(ls)
    sw = np.arange(512) ^ 1
    sw128 = np.arange(128) ^ 1
    wkv, wqg, wo = [], [], []
    for l in ls:
        w = w_in[l]
        aq, ak, av, ag = w[:, 0:512], w[:, 512:1024], w[:, 1024:1536], w[:, 1536:2048]
        bq, bk, bv, bg = w[:, 2048:2560], w[:, 2560:2688], w[:, 2688:2816], w[:, 2816:3328]
        bks = bk[:, sw128]
        bk2 = np.concatenate([bk[:, 0:64], bk[:, 0:64], bk[:, 64:128], bk[:, 64:128]], 1)
        bk2s = np.concatenate([bks[:, 0:64], bks[:, 0:64], bks[:, 64:128], bks[:, 64:128]], 1)
        wkv.append(np.concatenate([ak, bk2, bk2s, av, bv], 1))
        wqg.append(np.concatenate([aq, bq, bq[:, sw], ag, bg], 1))
        wo.append(w_out[l])
    g = np.zeros((128, 21 * nl), np.float32)
    j64 = np.arange(128) % 64
    for i, l in enumerate(ls):
        g[:, 8 * i:8 * i + 8] = pre_norm_g[l].reshape(8, 128).T
        g[:, 8 * nl + 8 * i:8 * nl + 8 * i + 8] = post_norm_g[l].reshape(8, 128).T
        g[:, 16 * nl + i] = diff_subln_g[l]
        g[:, 17 * nl + i] = q_norm_g[l][j64]
        g[:, 18 * nl + i] = q_norm_g[l][j64 ^ 1]
        g[:, 19 * nl + i] = k_norm_g[l][j64]
        g[:, 20 * nl + i] = k_norm_g[l][j64 ^ 1]
    lam_init = [0.8 - 0.6 * math.exp(-0.3 * l) for l in ls]
    lconst = np.array([lam_init + [1.0 - v for v in lam_init]], np.float32)
    dlam = np.concatenate([diff_lambda[l].reshape(1, 256) for l in ls], 1).astype(np.float32)
    return dict(wkv=np.ascontiguousarray(np.stack(wkv), dtype=np.float32), wqg=np.ascontiguousarray(np.stack(wqg), dtype=np.float32),
                wo=np.ascontiguousarray(np.stack(wo), dtype=np.float32), gains=g, lconst=lconst, dlam=np.ascontiguousarray(dlam))


def kernel(x, rel_bias, pre_norm_g, w_in, diff_lambda, diff_subln_g, q_norm_g, k_norm_g, w_out, post_norm_g):
    x = np.asarray(x, np.float32)
    args = [np.asarray(a, np.float32) for a in (pre_norm_g, w_in, diff_lambda, diff_subln_g, q_norm_g, k_norm_g, w_out, post_norm_g)]
    rel_bias = np.asarray(rel_bias, np.float32)
    stat = [_static_inputs(h, rel_bias) for h in range(2)]
    perms = [_local_perm(h) for h in range(2)]
    if FUSED:
        nc = _get_prog("fused", [0, 1], [8, 4], True)
        wi = _weight_inputs([0, 1], *args)
        in_maps = []
        for c in range(8):
            b, half = c // 2, c % 2
            m = dict(stat[half])
            m.update(wi)
            m["xT"] = np.ascontiguousarray(x[b][perms[half]].T)
            in_maps.append(m)
        res = run_bass_kernel_spmd(nc, in_maps, core_ids=list(range(8)))
        outp = np.empty_like(x)
        for c in range(8):
            b, half = c // 2, c % 2
            outp[b, half * SH:(half + 1) * SH, :] = res.results[c]["out"].T
        return outp
    cur = x
    nc = _get_prog("single", [0], [4], False)
    for l in range(L):
        wi = _weight_inputs([l], *args)
        in_maps = []
        for c in range(8):
            b, half = c // 2, c % 2
            m = dict(stat[half])
            m.update(wi)
            m["xT"] = np.ascontiguousarray(cur[b][perms[half]].T)
            in_maps.append(m)
        res = run_bass_kernel_spmd(nc, in_maps, core_ids=list(range(8)))
        nxt_x = np.empty_like(cur)
        for c in range(8):
            b, half = c // 2, c % 2
            nxt_x[b, half * SH:(half + 1) * SH, :] = res.results[c]["out"].T
        cur = nxt_x
    return cur
```

```python
import math
from contextlib import ExitStack
import numpy as np
import concourse.bass as bass
import concourse.mybir as mybir
from concourse.bass_utils import run_bass_kernel_spmd

F32 = mybir.dt.float32
BF16 = mybir.dt.bfloat16
AF = mybir.ActivationFunctionType
ALU = mybir.AluOpType
AX = mybir.AxisListType

ENGS = ("pe", "act", "dve", "pool", "sp")
BLOCKNAME = {"pe": "tensor", "act": "scalar", "dve": "vector", "pool": "gpsimd", "sp": "sync"}

FUSED = True
D = 1024
S = 4096
SH = 2048
L = 2
EPS = 1e-6
NKV = 1664
NQG = 2560


class Op:
    __slots__ = ("eng", "emit", "dma", "idx", "deps", "need_inc", "inc_val", "dma_val")


class Prog:
    def __init__(self, nc):
        self.nc = nc
        self.ops = []
        self.last_writer = {}
        self.readers = {}
        self.dma_counts = {}

    def op(self, eng, emit, reads=(), writes=(), dma=None):
        o = Op()
        o.eng, o.emit, o.dma, o.idx = eng, emit, dma, len(self.ops)
        o.need_inc, o.inc_val, o.dma_val = False, 0, 0
        deps = set()
        for r in reads:
            w = self.last_writer.get(r)
            if w is not None:
                deps.add(w)
        for w_ in writes:
            w = self.last_writer.get(w_)
            if w is not None:
                deps.add(w)
            rd = self.readers.get(w_)
            if rd:
                deps.update(rd.values())
        best, final = {}, set()
        for d in deps:
            dop = self.ops[d]
            if dop.dma is not None:
                final.add(d)
            elif dop.eng not in best or best[dop.eng] < d:
                best[dop.eng] = d
        final.update(best.values())
        o.deps = final
        for w_ in writes:
            self.last_writer[w_] = o.idx
            self.readers[w_] = {}
        for r in reads:
            if r in writes:
                continue
            rd = self.readers.setdefault(r, {})
            if dma is not None:
                rd[("dma", o.idx)] = o.idx
            else:
                rd[eng] = o.idx
        if dma is not None:
            self.dma_counts[dma] = self.dma_counts.get(dma, 0) + 16
            o.dma_val = self.dma_counts[dma]
        self.ops.append(o)
        return o

    def emit_all(self, final_wait_keys=()):
        nc, ops = self.nc, self.ops
        for o in ops:
            for d in o.deps:
                dep = ops[d]
                if dep.dma is None and not (dep.eng == "pe" and o.eng == "pe"):
                    dep.need_inc = True
        cnt = {e: 0 for e in ENGS}
        for o in ops:
            if o.dma is None and o.need_inc:
                cnt[o.eng] += 1
                o.inc_val = cnt[o.eng]
        by_eng = {e: [] for e in ENGS}
        for o in ops:
            by_eng[o.eng].append(o)
        with ExitStack() as st:
            engsem = {e: st.enter_context(nc.semaphore("s_" + e)) for e in ENGS}
            dmasem = {k: st.enter_context(nc.semaphore("d%d" % i)) for i, k in enumerate(self.dma_counts)}
            block = st.enter_context(nc.Block())

            def make_body(e):
                def body(eng):
                    known = {}
                    for o in by_eng[e]:
                        waits = {}
                        for d in o.deps:
                            dep = ops[d]
                            if dep.dma is not None:
                                sem, val = dmasem[dep.dma], dep.dma_val
                            else:
                                if dep.eng == "pe" and e == "pe":
                                    continue
                                sem, val = engsem[dep.eng], dep.inc_val
                            if waits.get(sem, 0) < val:
                                waits[sem] = val
                        for sem, val in waits.items():
                            if known.get(sem, 0) < val:
                                eng.wait_ge(sem, val)
                                known[sem] = val
                        ins = o.emit(eng)
                        if o.dma is not None:
                            ins.then_inc(dmasem[o.dma], 16)
                        elif o.need_inc:
                            ins.then_inc(engsem[e], 1)
                    if e == "sp":
                        for k in final_wait_keys:
                            eng.wait_ge(dmasem[k], self.dma_counts[k])
                return body

            for e in ENGS:
                if by_eng[e] or e == "sp":
                    getattr(block, BLOCKNAME[e])(make_body(e))


def t5_bucket_np(rel):
    nb, max_exact = 16, 8
    n = np.abs(rel)
    large = max_exact + (np.log(np.maximum(n, 1).astype(np.float32) / max_exact)
                         / math.log(128 / max_exact) * (nb - max_exact)).astype(np.int32)
    large = np.minimum(large, nb - 1)
    return np.where(rel > 0, nb, 0) + np.where(n < max_exact, n, large)


def near_kind(kc, qb):
    same = (kc < 16) == (qb < 4)
    if same:
        d = kc - 4 * qb
        if -1 <= d <= 4:
            return (d, 0)
        return None
    if (kc, qb) == (16, 3):
        return (4, 1)
    if (kc, qb) == (15, 4):
        return (-1, 1)
    if (kc, qb) == (31, 0):
        return (-1, 2)
    if (kc, qb) == (0, 7):
        return (4, 2)
    return None


def build_program(layers, nqbs, fused, debug=0):
    nc = bass.Bass("TRN2", target_bir_lowering=False)
    NL = len(layers)

    def din(name, shape):
        return nc.dram_tensor(name, shape, F32, kind="ExternalInput")

    xT_t = din("xT", [D, S])
    cos_t, sin_t = din("cosT", [128, S]), din("sinT", [128, S])
    wkv_t, wqg_t, wo_t = din("wkv", [NL, D, NKV]), din("wqg", [NL, D, NQG]), din("wo", [NL, D, D])
    NG = 21 * NL
    gains_t = din("gains", [128, NG])
    dlam_t = din("dlam", [1, NL * 256])
    lconst_t = din("lconst", [1, 2 * NL])
    wtb_t = din("wtb", [128, 4 * 1152])
    cbt_t = din("cbt", [1, 1024])
    cmat_t = din("cmat", [128, 512])
    out_t = nc.dram_tensor("out", [D, SH], F32, kind="ExternalOutput")
    if fused:
        x1_t = nc.dram_tensor("x1s", [D, S], F32)
    hbuf_t = [nc.dram_tensor("hbuf%d" % i, [D, S], BF16) for i in range(NL)]

    dbg = {}
    if debug:
        for nm, w in (("dKA", 4 * S), ("dKB", 2 * S), ("dVA", 32 * 512), ("dVB", 33 * 132), ("dR1", 20480), ("dR2", 20480), ("dR3", 20480)):
            dbg[nm] = nc.dram_tensor(nm, [128, w], BF16, kind="ExternalOutput")
        dbg["dHT"] = nc.dram_tensor("dHT", [128, 8 * 512], BF16, kind="ExternalOutput")
        dbg["dSM"] = nc.dram_tensor("dSM", [128, 64], F32, kind="ExternalOutput")
    P = Prog(nc)
    st = ExitStack()
    with st:
        def sb(name, shape, dt):
            return st.enter_context(nc.sbuf_tensor(name, shape, dt))

        ps = [st.enter_context(nc.psum_tensor("ps%d" % i, [128, 512], F32)) for i in range(8)]
        PK = ["ps%d" % i for i in range(8)]

        KA = sb("KA", [128, 4, S], BF16)
        KB = sb("KB", [128, 2, S], BF16)
        VA = sb("VA", [128, 32, 512], BF16)
        VB = sb("VB", [128, 33, 2, 66], BF16)
        VBF = VB[:].rearrange("p k g d -> p (k g d)")
        WT = sb("WT", [128, 4, 1152], BF16)
        CB = sb("CB", [128, 1024], F32)
        IDB = sb("IDB", [128, 384], BF16)
        ONES2 = sb("ONES2", [128, 128], F32)
        ONESF = sb("ONESF", [128, 128], F32)
        ONESB = sb("ONESB", [128, 128], BF16)
        GN = sb("GN", [128, NG], F32)
        DL = sb("DL", [128, max(512, NL * 256)], F32)
        LC = sb("LC", [128, 2 * NL], F32)
        SM = sb("SM", [128, 64], F32)
        LTMP = sb("LTMP", [128, 128], F32)
        R = sb("R", [128, 20480], BF16)
        WKV = R[:, 0:8 * NKV].rearrange("p (c n) -> p c n", c=8)
        QA = R[:, 0:4096].rearrange("p (h n) -> p h n", h=8)
        QB = R[:, 4096:8192].rearrange("p (h n) -> p h n", h=8)
        GA = R[:, 8192:10240].rearrange("p (h n) -> p h n", h=4)
        GB = R[:, 10240:14336].rearrange("p (h n) -> p h n", h=8)
        HQ = R[:, 14336:18432].rearrange("p (c n) -> p c n", c=8)
        EB = R[:, 18432:20480].rearrange("p (e n) -> p e n", e=4)
        HT = sb("HT", [128, 8, 512], BF16)
        HTa = R[:, 13312:17408].rearrange("p (c n) -> p c n", c=8)
        HTS = [(HT[:], "HT"), (HTa, "HTa")]
        YTb = HT[:].bitcast(F32)
        assert list(YTb.shape) == [128, 8, 256], YTb.shape
        XIN = sb("XIN", [128, 8, 256], F32)
        YT = sb("YT", [128, 8, 256], F32)
        H2 = sb("H2", [128, 8, 256], BF16)
        TMP = [sb("TMP%d" % i, [128, 512], F32) for i in range(6)]
        CSB = sb("CSB", [128, 2, 512], F32)
        WS = [sb("WS%d" % i, [128, 8, 128], BF16) for i in range(4)]
        WOA = [sb("WOA%d" % i, [128, 4, 128], BF16) for i in range(2)]
        WOB = [sb("WOB%d" % i, [128, 8, 128], BF16) for i in range(2)]

        O_PNG, O_POG, O_SUB, O_QN, O_QNS, O_KN, O_KNS = 0, 8 * NL, 16 * NL, 17 * NL, 18 * NL, 19 * NL, 20 * NL
        C_EPS, C_LAM, C_NLAM, C_SUBG, C_S = 0, 1, 1 + NL, 1 + 2 * NL, 1 + 3 * NL

        RKEYS = ["QA", "QB", "GA", "GB", "HQ", "EB0", "EB1", "EB2", "EB3", "HTa"] + ["WKV%d" % j for j in range(13)]
        rot = {}

        def nxt(name, n):
            i = rot.get(name, 0)
            rot[name] = i + 1
            return i % n

        POOLS = {"main": [(TMP[i], "TMP%d" % i) for i in range(6)],
                 "post": [(TMP[i], "TMP%d" % i) for i in range(3)],
                 "jit": [(TMP[i], "TMP%d" % i) for i in range(3, 6)] + [(DL, "DL")]}

        def tmp(pool="main"):
            lst = POOLS[pool]
            return lst[nxt("tmp_" + pool, len(lst))]

        P.op("sp", lambda e: e.dma_start(out=GN[:], in_=gains_t.ap()), writes=["GN"], dma="GN")
        P.op("sp", lambda e: e.dma_start(out=DL[:, 0:NL * 256], in_=bass.AP(dlam_t, 0, [[0, 128], [1, NL * 256]])), writes=["DL"], dma="DL")
        P.op("sp", lambda e: e.dma_start(out=LC[:], in_=bass.AP(lconst_t, 0, [[0, 128], [1, 2 * NL]])), writes=["LC"], dma="LC")
        P.op("sp", lambda e: e.dma_start(out=CB[:], in_=bass.AP(cbt_t, 0, [[0, 128], [1, 1024]])), writes=["CB"], dma="CB")
        P.op("sp", lambda e: e.dma_start(out=ONES2[:], in_=cmat_t.ap()[:, 384:512]), writes=["ONES2"], dma="ONES2")
        P.op("pool", lambda e: e.dma_start(out=IDB[:], in_=cmat_t.ap()[:, 0:384]), writes=["IDB"], dma="IDB")
        P.op("pool", lambda e: e.dma_start(out=WT[:].rearrange("p h n -> p (h n)"), in_=wtb_t.ap()), writes=["WT"], dma="WT")
        P.op("pool", lambda e: e.memset(ONESF[:], 1.0), writes=["ONESF"])
        P.op("pool", lambda e: e.memset(ONESB[:], 1.0), writes=["ONESB"])
        for wi_ in range(2):
            P.op("pool", lambda e, wi_=wi_: e.memset(WOB[wi_][:], 0.0), writes=["WOB%d" % wi_])
        P.op("pool", lambda e: e.memset(SM[:], 0.0), writes=["SM"])
        P.op("pool", lambda e: e.memset(SM[:, C_EPS:C_EPS + 1], EPS), reads=["SM"], writes=["SM"])
        P.op("pool", lambda e: e.memset(VB[:], 0.0), writes=["VB"])
        P.op("pool", lambda e: e.memset(VB[:, 0:32, :, 64:65], 1.0), writes=["VB"])
        EPSB = SM[:, C_EPS:C_EPS + 1]
        for li in range(NL):
            o = li * 256
            P.op("dve", lambda e, o=o: e.tensor_tensor(out=LTMP[:, 0:64], in0=DL[:, o:o + 64], in1=DL[:, o + 64:o + 128], op=ALU.mult),
                 reads=["DL"], writes=["LT0"])
            P.op("dve", lambda e, o=o: e.tensor_tensor(out=LTMP[:, 64:128], in0=DL[:, o + 128:o + 192], in1=DL[:, o + 192:o + 256], op=ALU.mult),
                 reads=["DL"], writes=["LT1"])
            c = C_S + 4 * li
            P.op("dve", lambda e, c=c: e.reduce_sum(out=SM[:, c:c + 1], in_=LTMP[:, 0:64], axis=AX.X), reads=["LT0", "SM"], writes=["SM"])
            P.op("dve", lambda e, c=c: e.reduce_sum(out=SM[:, c + 1:c + 2], in_=LTMP[:, 64:128], axis=AX.X), reads=["LT1", "SM"], writes=["SM"])
            P.op("act", lambda e, c=c: e.activation(out=SM[:, c + 2:c + 4], in_=SM[:, c:c + 2], func=AF.Exp), reads=["SM"], writes=["SM"])
            P.op("dve", lambda e, c=c, li=li: e.tensor_tensor(out=SM[:, C_LAM + li:C_LAM + li + 1], in0=SM[:, c + 2:c + 3],
                                                               in1=SM[:, c + 3:c + 4], op=ALU.subtract), reads=["SM"], writes=["SM"])
            P.op("dve", lambda e, li=li: e.tensor_tensor(out=SM[:, C_LAM + li:C_LAM + li + 1], in0=SM[:, C_LAM + li:C_LAM + li + 1],
                                                         in1=LC[:, li:li + 1], op=ALU.add), reads=["SM", "LC"], writes=["SM"])
            P.op("dve", lambda e, li=li: e.tensor_scalar_mul(out=SM[:, C_NLAM + li:C_NLAM + li + 1], in0=SM[:, C_LAM + li:C_LAM + li + 1],
                                                             scalar1=-1.0), reads=["SM"], writes=["SM"])
            P.op("dve", lambda e, li=li: e.tensor_tensor(out=SM[:, C_SUBG + li:C_SUBG + li + 1], in0=GN[:, O_SUB + li:O_SUB + li + 1],
                                                         in1=LC[:, NL + li:NL + li + 1], op=ALU.mult), reads=["SM", "LC", "GN"], writes=["SM"])

        def rstd_from(ps_ap, pskey, inv_n, n=512, pool="main"):
            t1, k1 = tmp(pool)
            P.op("act", lambda e: e.activation(out=t1[:, 0:n], in_=ps_ap, func=AF.Ln, bias=EPSB, scale=inv_n),
                 reads=["SM"], writes=[pskey, k1])
            P.op("act", lambda e: e.activation(out=t1[:, 0:n], in_=t1[:, 0:n], func=AF.Exp, scale=-0.5), writes=[k1])
            return t1, k1

        def norm_block(src_ap_fn, srckey, gcol0, dst, dstkey, n, dst_slice, pool="main", bank=None):
            b = (nxt("psN", 2) + 6) if bank is None else bank
            for c in range(8):
                t, k = tmp(pool)
                P.op("act", lambda e, c=c, t=t: e.activation(out=t[:, 0:n], in_=src_ap_fn(c), func=AF.Square),
                     reads=[srckey], writes=[k])
                P.op("pe", lambda e, c=c, t=t: e.matmul(ps[b][:, 0:n], lhsT=ONESF[:], rhs=t[:, 0:n], start=(c == 0), stop=(c == 7)),
                     reads=[k, "ONESF"], writes=[PK[b]])
            rs, rk = rstd_from(ps[b][:, 0:n], PK[b], 1.0 / D, n, pool)
            for c in range(8):
                P.op("dve", lambda e, c=c: e.scalar_tensor_tensor(out=dst[:, c, dst_slice], in0=src_ap_fn(c),
                                                                   scalar=GN[:, gcol0 + c:gcol0 + c + 1], in1=rs[:, 0:n],
                                                                   op0=ALU.mult, op1=ALU.mult),
                     reads=[srckey, rk, "GN"], writes=[dstkey])

        def rope_chunk(psA, kA, psB, kB, gcol, gscol, tok0, scale, dst_ap, dstkey, pool="main", ssbank=None):
            sq, ksq = tmp(pool)
            P.op("act", lambda e: e.activation(out=sq[:], in_=psA[:], func=AF.Square), writes=[kA, ksq])
            bss = (nxt("psS", 2) + 6) if ssbank is None else ssbank
            P.op("pe", lambda e: e.matmul(ps[bss][:], lhsT=ONES2[:], rhs=sq[:], start=True, stop=True),
                 reads=[ksq, "ONES2"], writes=[PK[bss]])
            rs, rk = rstd_from(ps[bss][:], PK[bss], 1.0 / 64, 512, pool)
            P.op("sp", lambda e: e.dma_start(out=CSB[:, 0, :], in_=cos_t.ap()[:, tok0:tok0 + 512]), writes=["CS0"], dma="CS0")
            P.op("sp", lambda e: e.dma_start(out=CSB[:, 1, :], in_=sin_t.ap()[:, tok0:tok0 + 512]), writes=["CS1"], dma="CS1")
            t1, k1 = tmp(pool)
            P.op("dve", lambda e: e.scalar_tensor_tensor(out=t1[:], in0=psA[:], scalar=GN[:, gcol:gcol + 1], in1=CSB[:, 0, :],
                                                         op0=ALU.mult, op1=ALU.mult), reads=["GN", "CS0"], writes=[kA, k1])
            t2, k2 = tmp(pool)
            P.op("dve", lambda e: e.scalar_tensor_tensor(out=t2[:], in0=psB[:], scalar=GN[:, gscol:gscol + 1], in1=CSB[:, 1, :],
                                                         op0=ALU.mult, op1=ALU.mult), reads=["GN", "CS1"], writes=[kB, k2])
            P.op("dve", lambda e: e.tensor_tensor(out=t1[:], in0=t1[:], in1=t2[:], op=ALU.add), reads=[k2], writes=[k1])
            for psl, dap in dst_ap:
                P.op("dve", lambda e, psl=psl, dap=dap: e.scalar_tensor_tensor(out=dap, in0=t1[psl, :], scalar=scale, in1=rs[psl, :],
                                                                               op0=ALU.mult, op1=ALU.mult),
                     reads=[k1, rk], writes=[dstkey])

        def load_w_chunk(src_t, li, col0, width=128):
            i = nxt("ws", 4)
            P.op("pool", lambda e: e.dma_start(out=WS[i][:, :, 0:width],
                                               in_=src_t.ap()[li].rearrange("(c p) n -> p c n", p=128)[:, :, col0:col0 + width]),
                 writes=["WS%d" % i], dma="WS%d" % i)
            return WS[i], "WS%d" % i

        def proj_fm(w, wkey, wcol0, rhs, rhskey, bank, m=128, mcol=0):
            for c in range(8):
                P.op("pe", lambda e, c=c: e.matmul(ps[bank][0:m, :], lhsT=w[:, c, wcol0 + mcol:wcol0 + mcol + m], rhs=rhs[:, c, :],
                                                   start=(c == 0), stop=(c == 7)),
                     reads=[wkey, rhskey], writes=[PK[bank]])

        def layer(i):
            li = i
            nqb = nqbs[i]
            first = (i == 0)
            last = (i == NL - 1)
            x_src = xT_t if first else x1_t
            P.op("pool", lambda e: e.memset(SM[:, 60:61], 0.0), writes=RKEYS + ["SM60"])
            for j in range(13):
                P.op("pool", lambda e, j=j: e.dma_start(out=WKV[:, :, j * 128:(j + 1) * 128],
                                                        in_=wkv_t.ap()[li].rearrange("(c p) n -> p c n", p=128)[:, :, j * 128:(j + 1) * 128]),
                     writes=["WKV%d" % j], dma="WKV%d" % j)
            def prep(tb):
                t0 = tb * 512
                HTc, HTk = HTS[tb % 2]
                th = []
                if first:
                    def half(hf):
                        tt = t0 + hf * 256
                        XB, XK = ((XIN, "XIN"), (YT, "YT"))[hf]
                        P.op("sp", lambda e: e.dma_start(out=XB[:], in_=xT_t.ap().rearrange("(c p) t -> p c t", p=128)[:, :, tt:tt + 256]),
                             writes=[XK], dma=XK)
                        norm_block(lambda c: XB[:, c, :], XK, O_PNG + 8 * li, HTc, HTk, 256, slice(hf * 256, hf * 256 + 256))
                        if hf == 1:
                            P.op("sp", lambda e: e.dma_start(out=hbuf_t[i].ap().rearrange("(c p) t -> p c t", p=128)[:, :, t0:t0 + 512], in_=HTc),
                                 reads=[HTk], writes=["hbuf%d_%d" % (i, tb)], dma="HTst_" + HTk)
                    th.append(lambda: half(0))
                    th.append(lambda: half(1))
                else:
                    th.append(lambda: P.op("sp", lambda e: e.dma_start(out=HTc, in_=hbuf_t[i].ap().rearrange("(c p) t -> p c t", p=128)[:, :, t0:t0 + 512]),
                                           reads=["hbuf%d_%d" % (i, tb)], writes=[HTk], dma=HTk))
                return th

            def projs(tb):
                t0 = tb * 512
                HTc, HTk = HTS[tb % 2]
                th = []

                def ka(h):
                    b = nxt("psP", 4)
                    proj_fm(WKV, "WKV%d" % h, h * 128, HTc, HTk, b)
                    if h % 2 == 0:
                        P.op("dve", lambda e: e.tensor_copy(out=KA[:, h, t0:t0 + 512], in_=ps[b][:]), writes=[PK[b], "KA"])
                    else:
                        P.op("act", lambda e: e.activation(out=KA[:, h, t0:t0 + 512], in_=ps[b][:], func=AF.Copy), writes=[PK[b], "KA"])

                def kb(g):
                    bA = nxt("psP", 4)
                    proj_fm(WKV, "WKV%d" % (4 + g), 512 + g * 128, HTc, HTk, bA)
                    bB = nxt("psP", 4)
                    proj_fm(WKV, "WKV%d" % (6 + g), 768 + g * 128, HTc, HTk, bB)
                    rope_chunk(ps[bA], PK[bA], ps[bB], PK[bB], O_KN + li, O_KNS + li, t0, 1.0, [(slice(0, 128), KB[:, g, t0:t0 + 512])], "KB")

                def vv(sub):
                    kc = tb * 4 + sub
                    b = nxt("psV", 2) + 4
                    for c in range(8):
                        P.op("pe", lambda e, c=c: e.matmul(ps[b][:], lhsT=HTc[:, c, sub * 128:(sub + 1) * 128], rhs=WKV[:, c, 1024:1536],
                                                           start=(c == 0), stop=(c == 7)), reads=[HTk, "WKV8", "WKV9", "WKV10", "WKV11"], writes=[PK[b]])
                    P.op("act", lambda e: e.activation(out=VA[:, kc, :], in_=ps[b][:], func=AF.Copy), writes=[PK[b], "VA"])
                    b2 = nxt("psV", 2) + 4
                    for c in range(8):
                        P.op("pe", lambda e, c=c: e.matmul(ps[b2][:, 0:128], lhsT=HTc[:, c, sub * 128:(sub + 1) * 128], rhs=WKV[:, c, 1536:1664],
                                                           start=(c == 0), stop=(c == 7)), reads=[HTk, "WKV12"], writes=[PK[b2]])
                    P.op("dve", lambda e: e.tensor_copy(out=VB[:, kc, :, 0:64], in_=ps[b2][:, 0:128].rearrange("p (g d) -> p g d", g=2)),
                         writes=[PK[b2], "VB"])
                for h in range(4):
                    th.append(lambda h=h: ka(h))
                for g in range(2):
                    th.append(lambda g=g: kb(g))
                for sub in range(4):
                    th.append(lambda sub=sub: vv(sub))
                return th

            def interleave0(a_, b_):
                ia = ib = 0
                while ia < len(a_) or ib < len(b_):
                    if ib >= len(b_) or (ia < len(a_) and ia * len(b_) <= ib * len(a_)):
                        a_[ia]()
                        ia += 1
                    else:
                        b_[ib]()
                        ib += 1
            interleave0(prep(0), [])
            for tb in range(8):
                interleave0(projs(tb), prep(tb + 1) if tb + 1 < 8 else [])
            if debug:
                P.op("sp", lambda e: e.dma_start(out=dbg["dKA"].ap(), in_=KA[:].rearrange("p h n -> p (h n)")), reads=["KA"], dma="dbg")
                P.op("sp", lambda e: e.dma_start(out=dbg["dKB"].ap(), in_=KB[:].rearrange("p h n -> p (h n)")), reads=["KB"], dma="dbg")
                P.op("sp", lambda e: e.dma_start(out=dbg["dVA"].ap(), in_=VA[:].rearrange("p h n -> p (h n)")), reads=["VA"], dma="dbg")
                P.op("sp", lambda e: e.dma_start(out=dbg["dVB"].ap(), in_=VBF), reads=["VB"], dma="dbg")
                P.op("sp", lambda e: e.dma_start(out=dbg["dHT"].ap(), in_=HTa.rearrange("p h n -> p (h n)")), reads=["HTa"], dma="dbg")
                P.op("sp", lambda e: e.dma_start(out=dbg["dSM"].ap(), in_=SM[:]), reads=["SM"], dma="dbg")
            P.op("pool", lambda e: e.memset(SM[:, 61:62], 0.0), writes=RKEYS + ["SM61"])
            P.op("pool", lambda e: e.memset(R[:, 0:8192], 0.0), writes=["QA", "QB"])
            P.op("pool", lambda e: e.memset(GB[64:128, :, :], 0.0), writes=["GB"])

            def jit(qb):
                q0 = qb * 512
                s1, s2 = [], []

                def bq(j):
                    if j == 0:
                        P.op("sp", lambda e: e.dma_start(out=HQ, in_=hbuf_t[i].ap().rearrange("(c p) t -> p c t", p=128)[:, :, q0:q0 + 512]),
                             reads=["hbuf%d_%d" % (i, qb)], writes=["HQ"], dma="HQ")
                    w, wk = load_w_chunk(wqg_t, li, 512 + j * 128)
                    bA = nxt("psJ", 4) + 2
                    proj_fm(w, wk, 0, HQ, "HQ", bA)
                    w2, wk2 = load_w_chunk(wqg_t, li, 1024 + j * 128)
                    bB = nxt("psJ", 4) + 2
                    proj_fm(w2, wk2, 0, HQ, "HQ", bB)
                    rope_chunk(ps[bA], PK[bA], ps[bB], PK[bB], O_QN + li, O_QNS + li, q0, 0.125,
                               [(slice(0, 64), QB[0:64, 2 * j, :]), (slice(64, 128), QB[64:128, 2 * j + 1, :])], "QB",
                               pool="jit", ssbank=6)

                def aq(h):
                    w, wk = load_w_chunk(wqg_t, li, h * 128)
                    b = nxt("psJ", 4) + 2
                    proj_fm(w, wk, 0, HQ, "HQ", b)
                    P.op("dve", lambda e: e.tensor_scalar_mul(out=QA[0:64, 2 * h, :], in0=ps[b][0:64, :], scalar1=0.125),
                         writes=[PK[b], "QA"])
                    P.op("dve", lambda e: e.tensor_scalar_mul(out=QA[64:128, 2 * h + 1, :], in0=ps[b][64:128, :], scalar1=0.125),
                         writes=[PK[b], "QA"])

                def ga(h):
                    w, wk = load_w_chunk(wqg_t, li, 1536 + h * 128)
                    b = nxt("psJ", 4) + 2
                    proj_fm(w, wk, 0, HQ, "HQ", b)
                    P.op("act", lambda e: e.activation(out=GA[:, h, :], in_=ps[b][:], func=AF.Silu), writes=[PK[b], "GA"])

                def gb(j):
                    w, wk = load_w_chunk(wqg_t, li, 2048 + j * 128)
                    for hh in range(2):
                        b = nxt("psJ", 4) + 2
                        proj_fm(w, wk, 0, HQ, "HQ", b, m=64, mcol=hh * 64)
                        P.op("act", lambda e, hh=hh, b=b: e.activation(out=GB[0:64, 2 * j + hh, :], in_=ps[b][0:64, :], func=AF.Silu),
                             writes=[PK[b], "GB"])
                for j in range(4):
                    s1.append(lambda j=j: bq(j))
                    s1.append(lambda j=j: aq(j))
                for h in range(4):
                    s2.append(lambda h=h: ga(h))
                for j in range(4):
                    s2.append(lambda j=j: gb(j))
                return s1, s2

            def post(qb):
                q0 = qb * 512
                s1, s2 = [], []

                def outproj(n):
                    wi = nxt("wo", 2)
                    P.op("pool", lambda e: e.dma_start(
                        out=WOA[wi][:], in_=wo_t.ap()[li][0:512, :].rearrange("(h p) n -> p h n", p=128)[:, :, n * 128:(n + 1) * 128]),
                        writes=["WOA%d" % wi], dma="WOA%d" % wi)
                    P.op("pool", lambda e: e.dma_start(
                        out=WOB[wi][0:64, :, :], in_=wo_t.ap()[li][512:1024, :].rearrange("(h p) n -> p h n", p=64)[:, :, n * 128:(n + 1) * 128]),
                        writes=["WOB%d" % wi], dma="WOB%d" % wi)
                    b = nxt("psO", 2)
                    for h in range(4):
                        P.op("pe", lambda e, h=h: e.matmul(ps[b][:], lhsT=WOA[wi][:, h, :], rhs=GA[:, h, :], start=(h == 0), stop=False),
                             reads=["WOA%d" % wi, "GA"], writes=[PK[b]])
                    for hq in range(8):
                        P.op("pe", lambda e, hq=hq: e.matmul(ps[b][:], lhsT=WOB[wi][:, hq, :], rhs=GB[:, hq, :], start=False, stop=(hq == 7)),
                             reads=["WOB%d" % wi, "GB"], writes=[PK[b]])
                    P.op("dve", lambda e: e.tensor_copy(out=YT[:, n, :], in_=ps[b][:, 0:256]), writes=[PK[b], "YT"])
                    P.op("act", lambda e: e.activation(out=YTb[:, n, :], in_=ps[b][:, 256:512], func=AF.Copy), writes=[PK[b], "HT"])

                def half_a(hf, st):
                    tt = q0 + hf * 256
                    Y, YK = ((YT[:], "YT"), (YTb, "HT"))[hf]
                    P.op("sp", lambda e: e.dma_start(out=XIN[:], in_=x_src.ap().rearrange("(c p) t -> p c t", p=128)[:, :, tt:tt + 256]),
                         reads=(["x1_%d" % qb] if not first else []), writes=["XIN"], dma="XIN")
                    for c in range(8):
                        t, k = tmp("post")
                        P.op("act", lambda e, c=c, t=t: e.activation(out=t[:, 0:256], in_=Y[:, c, :], func=AF.Square), reads=[YK], writes=[k])
                        P.op("pe", lambda e, c=c, t=t: e.matmul(ps[7][:, 0:256], lhsT=ONESF[:], rhs=t[:, 0:256], start=(c == 0), stop=(c == 7)),
                             reads=[k, "ONESF"], writes=[PK[7]])
                    st["rs"] = rstd_from(ps[7][:, 0:256], PK[7], 1.0 / D, 256, "post")

                def half_b(hf, st):
                    tt = q0 + hf * 256
                    Y, YK = ((YT[:], "YT"), (YTb, "HT"))[hf]
                    rs, rk = st["rs"]
                    for c in range(8):
                        P.op("dve", lambda e, c=c: e.scalar_tensor_tensor(out=Y[:, c, :], in0=Y[:, c, :],
                                                                          scalar=GN[:, O_POG + 8 * li + c:O_POG + 8 * li + c + 1],
                                                                          in1=rs[:, 0:256], op0=ALU.mult, op1=ALU.mult),
                             reads=[rk, "GN"], writes=[YK])
                        P.op("dve", lambda e, c=c: e.tensor_tensor(out=Y[:, c, :], in0=Y[:, c, :], in1=XIN[:, c, :], op=ALU.add),
                             reads=["XIN"], writes=[YK])
                    if last:
                        P.op("sp", lambda e: e.dma_start(out=out_t.ap().rearrange("(c p) t -> p c t", p=128)[:, :, tt:tt + 256], in_=Y),
                             reads=[YK], dma="OUT_" + YK)
                    else:
                        P.op("sp", lambda e: e.dma_start(out=x1_t.ap().rearrange("(c p) t -> p c t", p=128)[:, :, tt:tt + 256], in_=Y),
                             reads=[YK], writes=["x1_%d" % qb], dma="X1st_" + YK)

                def half_c(hf):
                    tt = q0 + hf * 256
                    Y, YK = ((YT[:], "YT"), (YTb, "HT"))[hf]
                    norm_block(lambda c: Y[:, c, :], YK, O_PNG + 8 * (li + 1), H2, "H2", 256, slice(0, 256), pool="post", bank=7)
                    P.op("sp", lambda e: e.dma_start(out=hbuf_t[i + 1].ap().rearrange("(c p) t -> p c t", p=128)[:, :, tt:tt + 256], in_=H2[:]),
                         reads=["H2"], writes=["hbuf%d_%d" % (i + 1, qb)], dma="H2st")
                for n in range(8):
                    s1.append(lambda n=n: outproj(n))
                for hf in range(2):
                    st = {}
                    s2.append(lambda hf=hf, st=st: half_a(hf, st))
                    s2.append(lambda hf=hf, st=st: half_b(hf, st))
                    if not last:
                        s2.append(lambda hf=hf: half_c(hf))
                return s1, s2

            def interleave(a, b):
                ia = ib = 0
                while ia < len(a) or ib < len(b):
                    if ib >= len(b) or (ia < len(a) and ia * len(b) <= ib * len(a)):
                        a[ia]()
                        ia += 1
                    else:
                        b[ib]()
                        ib += 1

            def attention(qb):
                tiles = []
                for h in range(4):
                    for kc in range(32):
                        for c in range(2):
                            tiles.append(("A", h, kc, c))
                for hq in range(8):
                    for kc in range(32):
                        tiles.append(("B", hq, kc, 0))
                T = len(tiles)
                sbank = [0] * T
                ebuf = [0] * T
                look = [2 if (k_ == "A" or h_ == 0) else 3 for (k_, h_, _kc, _c) in tiles]
                last_user = {4: -10, 5: -10, 6: -10, 7: -10}

                def pick_bank(t):
                    allowed = (4, 5, 6, 7) if look[t] == 3 else (4, 5, 6)
                    bnk = min(allowed, key=lambda x_: last_user[x_])
                    assert last_user[bnk] <= t - look[t] - 1, (t, bnk, last_user)
                    last_user[bnk] = t
                    return bnk
                deferred = []

                def defer(n, fn):
                    deferred.append([n, fn])

                def tick():
                    fire = [d for d in deferred if d[0] <= 0]
                    for d in fire:
                        deferred.remove(d)
                    for d in deferred:
                        d[0] -= 1
                    for d in fire:
                        d[1]()

                def rec_qk(t):
                    kind, hx, kc, c = tiles[t]
                    sbk = pick_bank(t)
                    sbank[t] = sbk
                    if kind == "A":
                        nk = near_kind(kc, qb)
                        pr = slice(c * 64, c * 64 + 64)
                        P.op("pe", lambda e: e.matmul(ps[sbk][:], lhsT=KA[:, hx, kc * 128:(kc + 1) * 128], rhs=QA[:, 2 * hx + c, :],
                                                      start=True, stop=(nk is None)), reads=["KA", "QA"], writes=[PK[sbk]])
                        if nk is not None:
                            d, sel = nk
                            u0 = (4 - d) * 128
                            P.op("pe", lambda e: e.matmul(ps[sbk][:], lhsT=IDB[:, sel * 128:(sel + 1) * 128], rhs=WT[:, hx, u0:u0 + 512],
                                                          start=False, stop=True), reads=["IDB", "WT"], writes=[PK[sbk]])
                    else:
                        g, j, hh = hx // 4, hx // 2, hx % 2
                        pr = slice(hh * 64, hh * 64 + 64)
                        P.op("pe", lambda e: e.matmul(ps[sbk][:], lhsT=KB[:, g, kc * 128:(kc + 1) * 128], rhs=QB[:, hx, :],
                                                      start=True, stop=True), reads=["KB", "QB"], writes=[PK[sbk]])

                def rec_exp(t):
                    kind, hx, kc, c = tiles[t]
                    sbk = sbank[t]
                    ei = nxt("eb", 4)
                    ebuf[t] = ei
                    if kind == "A":
                        ci = (hx * 32 + kc) * 8 + qb
                        P.op("act", lambda e: e.activation(out=EB[:, ei, :], in_=ps[sbk][:], func=AF.Exp, bias=CB[:, ci:ci + 1], scale=1.0),
                             reads=["CB"], writes=[PK[sbk], "EB%d" % ei])
                    else:
                        P.op("act", lambda e: e.activation(out=EB[:, ei, :], in_=ps[sbk][:], func=AF.Exp), writes=[PK[sbk], "EB%d" % ei])

                def fin_A(h):
                    o0, k0 = tmp()
                    o1, k1 = tmp()
                    l0, kl0 = tmp()
                    l1, kl1 = tmp()
                    P.op("act", lambda e: e.activation(out=l0[:], in_=ps[2][:], func=AF.Ln), writes=[PK[2], kl0])
                    P.op("act", lambda e: e.activation(out=l1[:], in_=ps[3][:], func=AF.Ln), writes=[PK[3], kl1])
                    P.op("dve", lambda e: e.tensor_copy(out=o0[:], in_=ps[0][:]), writes=[PK[0], k0])
                    P.op("dve", lambda e: e.tensor_copy(out=o1[:], in_=ps[1][:]), writes=[PK[1], k1])

                    def s1():
                        P.op("act", lambda e: e.activation(out=l0[:], in_=l0[:], func=AF.Exp, scale=-1.0), writes=[kl0])
                        P.op("act", lambda e: e.activation(out=l1[:], in_=l1[:], func=AF.Exp, scale=-1.0), writes=[kl1])
                        P.op("dve", lambda e: e.tensor_tensor(out=o0[:], in0=o0[:], in1=l0[:], op=ALU.mult), reads=[kl0], writes=[k0])
                        P.op("dve", lambda e: e.tensor_tensor(out=o1[:], in0=o1[:], in1=l1[:], op=ALU.mult), reads=[kl1], writes=[k1])
                        P.op("dve", lambda e: e.scalar_tensor_tensor(out=o0[:], in0=o1[:], scalar=SM[:, C_NLAM + li:C_NLAM + li + 1],
                                                                     in1=o0[:], op0=ALU.mult, op1=ALU.add), reads=[k1, "SM"], writes=[k0])
                        P.op("dve", lambda e: e.tensor_tensor(out=o1[:], in0=o0[:], in1=o0[:], op=ALU.mult), reads=[k0], writes=[k1])

                    def s2():
                        P.op("pe", lambda e: e.matmul(ps[7][:], lhsT=ONESF[:], rhs=o1[:], start=True, stop=True),
                             reads=[k1, "ONESF"], writes=[PK[7]])

                    def s3():
                        P.op("act", lambda e: e.activation(out=l0[:], in_=ps[7][:], func=AF.Ln, bias=EPSB, scale=1.0 / 128),
                             reads=["SM"], writes=[PK[7], kl0])
                        P.op("act", lambda e: e.activation(out=l0[:], in_=l0[:], func=AF.Exp, scale=-0.5), writes=[kl0])

                    def s4():
                        P.op("dve", lambda e: e.scalar_tensor_tensor(out=o0[:], in0=o0[:], scalar=SM[:, C_SUBG + li:C_SUBG + li + 1],
                                                                     in1=l0[:], op0=ALU.mult, op1=ALU.mult), reads=[kl0, "SM"], writes=[k0])
                        P.op("dve", lambda e: e.tensor_tensor(out=GA[:, h, :], in0=o0[:], in1=GA[:, h, :], op=ALU.mult), reads=[k0], writes=["GA"])
                    defer(3, s1)
                    defer(8, s2)
                    defer(12, s3)
                    defer(16, s4)

                def fin_B(hq):
                    ob = hq % 2
                    bb = 2 + (hq % 2)
                    rr, rrk = tmp()
                    rb, rbk = tmp()
                    P.op("dve", lambda e: e.tensor_copy(out=rb[0:64, :], in_=ps[ob][0:64, :]), writes=[PK[ob], rbk])
                    P.op("dve", lambda e: e.reciprocal(out=rr[64:65, :], in_=ps[ob][64:65, :]), writes=[PK[ob], rrk])

                    def s2():
                        P.op("pe", lambda e: e.matmul(ps[bb][0:64, :], lhsT=ONESF[64:65, 0:64], rhs=rr[64:65, :], start=True, stop=True),
                             reads=[rrk, "ONESF"], writes=[PK[bb]])

                    def s3():
                        P.op("dve", lambda e: e.tensor_tensor(out=rb[0:64, :], in0=ps[bb][0:64, :], in1=rb[0:64, :], op=ALU.mult),
                             writes=[PK[bb], rbk])
                        P.op("dve", lambda e: e.tensor_tensor(out=GB[0:64, hq, :], in0=rb[0:64, :], in1=GB[0:64, hq, :], op=ALU.mult),
                             reads=[rbk], writes=["GB"])
                    defer(10, s2)
                    defer(14, s3)

                def rec_pv(t):
                    kind, hx, kc, c = tiles[t]
                    ei = ebuf[t]
                    if kind == "A":
                        P.op("pe", lambda e: e.matmul(ps[c][:], lhsT=VA[:, kc, hx * 128:(hx + 1) * 128], rhs=EB[:, ei, :],
                                                      start=(kc == 0), stop=(kc == 31)), reads=["VA", "EB%d" % ei], writes=[PK[c]])
                        P.op("pe", lambda e: e.matmul(ps[2 + c][:], lhsT=ONESB[:], rhs=EB[:, ei, :],
                                                      start=(kc == 0), stop=(kc == 31)), reads=["ONESB", "EB%d" % ei], writes=[PK[2 + c]])
                        if kc == 31 and c == 1:
                            fin_A(hx)
                    else:
                        g = hx // 4
                        ob = hx % 2
                        off = (kc * 2 + g) * 66
                        P.op("pe", lambda e: e.matmul(ps[ob][:], lhsT=VBF[:, off:off + 128], rhs=EB[:, ei, :],
                                                      start=(kc == 0), stop=(kc == 31)), reads=["VB", "EB%d" % ei], writes=[PK[ob]])
                        if kc == 31:
                            fin_B(hx)

                qn = 0
                for e_ in range(T):
                    while qn < T and qn <= e_ + look[qn]:
                        rec_qk(qn)
                        qn += 1
                    rec_exp(e_)
                    rec_pv(e_)
                    tick()
                while deferred:
                    tick()
                if debug and qb == 0:
                    P.op("sp", lambda e: e.dma_start(out=dbg["dR2"].ap(), in_=R[:]), reads=["QA", "QB", "GA", "GB", "HQ"], dma="dbg")
                    P.op("sp", lambda e: e.dma_start(out=dbg["dR3"].ap(), in_=R[:]), reads=["QA", "QB", "GA", "GB", "HQ"], dma="dbg")
                    P.op("sp", lambda e: e.dma_start(out=dbg["dR1"].ap(), in_=R[:]), reads=["QA", "QB", "GA", "GB", "HQ"], dma="dbg")

            j1, j2 = jit(0)
            interleave(j1 + j2, [])
            for qb in range(nqb):
                attention(qb)
                p1, p2 = post(qb)
                if qb + 1 < nqb:
                    j1, j2 = jit(qb + 1)
                    interleave(p1, j1)
                    interleave(p2, j2)
                else:
                    interleave(p1 + p2, [])

        for i in range(NL):
            layer(i)
        P.emit_all(final_wait_keys=["OUT_YT", "OUT_HT"] + (["dbg"] if debug else []))
    return nc


_PROG_CACHE = {}


def _get_prog(key, layers, nqbs, fused):
    if key not in _PROG_CACHE:
        _PROG_CACHE[key] = build_program(layers, nqbs, fused)
    return _PROG_CACHE[key]


def _local_perm(half):
    own = np.arange(half * SH, half * SH + SH)
    oth = np.arange((1 - half) * SH, (1 - half) * SH + SH)
    return np.concatenate([own, oth])


def _rope_tables(pos):
    inv = (10000.0 ** (-np.arange(0, 32, 2, dtype=np.float32) / np.float32(32))).astype(np.float32)
    row = (pos // 64).astype(np.float32)
    col = (pos % 64).astype(np.float32)
    ang = np.concatenate([row[:, None] * inv, col[:, None] * inv], axis=-1).astype(np.float32)
    c, s = np.cos(ang).astype(np.float32), np.sin(ang).astype(np.float32)
    j = np.arange(64)
    cosT = c[:, j // 2].T
    sgn = np.where(j % 2 == 0, -1.0, 1.0).astype(np.float32)
    sinT = (s[:, j // 2] * sgn[None, :]).T
    return (np.ascontiguousarray(np.concatenate([cosT, cosT], 0), dtype=np.float32),
            np.ascontiguousarray(np.concatenate([sinT, sinT], 0), dtype=np.float32))


def _static_inputs(half, rel_bias):
    perm = _local_perm(half)
    cosT, sinT = _rope_tables(perm)
    p = np.arange(128)[:, None]
    u = np.arange(1152)[None, :]
    bk = t5_bucket_np((p - u + 512).astype(np.int32))
    wtb = np.concatenate([rel_bias[bk, h] for h in range(4)], axis=1).astype(np.float32)
    cbt = np.zeros((4, 32, 8), np.float32)
    for kc in range(32):
        kp = perm[kc * 128:(kc + 1) * 128]
        for qb in range(8):
            qp = perm[qb * 512:(qb + 1) * 512]
            relmin, relmax = kp.min() - qp.max(), kp.max() - qp.min()
            nk = near_kind(kc, qb)
            if relmin >= 91:
                bkt = 31
            elif relmax <= -91:
                bkt = 15
            else:
                bkt = None
                assert nk is not None and (nk[1] == 0 or nk[1] == half + 1), (kc, qb, half)
                assert nk[0] * 128 == kp.min() - qp.min()
            if bkt is not None and nk is not None and (nk[1] == 0 or nk[1] == half + 1):
                bkt = None
                assert nk[0] * 128 == kp.min() - qp.min()
            if bkt is not None:
                cbt[:, kc, qb] = rel_bias[bkt, :]
    eye = np.eye(128, dtype=np.float32)
    z = np.zeros((128, 128), np.float32)
    ones2 = np.zeros((128, 128), np.float32)
    ones2[:64, :64] = 1.0
    ones2[64:, 64:] = 1.0
    cmat = np.concatenate([eye, eye if half == 0 else z, eye if half == 1 else z, ones2], axis=1)
    return dict(cosT=cosT, sinT=sinT, wtb=np.ascontiguousarray(wtb), cbt=np.ascontiguousarray(cbt.reshape(1, 1024)),
                cmat=np.ascontiguousarray(cmat))


def _weight_inputs(ls, pre_norm_g, w_in, diff_lambda, diff_subln_g, q_norm_g, k_norm_g, w_out, post_norm_g):
    nl = len(ls)
    sw = np.arange(512) ^ 1
    sw128 = np.arange(128) ^ 1
    wkv, wqg, wo = [], [], []
    for l in ls:
        w = w_in[l]
        aq, ak, av, ag = w[:, 0:512], w[:, 512:1024], w[:, 1024:1536], w[:, 1536:2048]
        bq, bk, bv, bg = w[:, 2048:2560], w[:, 2560:2688], w[:, 2688:2816], w[:, 2816:3328]
        bks = bk[:, sw128]
        bk2 = np.concatenate([bk[:, 0:64], bk[:, 0:64], bk[:, 64:128], bk[:, 64:128]], 1)
        bk2s = np.concatenate([bks[:, 0:64], bks[:, 0:64], bks[:, 64:128], bks[:, 64:128]], 1)
        wkv.append(np.concatenate([ak, bk2, bk2s, av, bv], 1))
        wqg.append(np.concatenate([aq, bq, bq[:, sw], ag, bg], 1))
        wo.append(w_out[l])
    g = np.zeros((128, 21 * nl), np.float32)
    j64 = np.arange(128) % 64
    for i, l in enumerate(ls):
        g[:, 8 * i:8 * i + 8] = pre_norm_g[l].reshape(8, 128).T
        g[:, 8 * nl + 8 * i:8 * nl + 8 * i + 8] = post_norm_g[l].reshape(8, 128).T
        g[:, 16 * nl + i] = diff_subln_g[l]
        g[:, 17 * nl + i] = q_norm_g[l][j64]
        g[:, 18 * nl + i] = q_norm_g[l][j64 ^ 1]
        g[:, 19 * nl + i] = k_norm_g[l][j64]
        g[:, 20 * nl + i] = k_norm_g[l][j64 ^ 1]
    lam_init = [0.8 - 0.6 * math.exp(-0.3 * l) for l in ls]
    lconst = np.array([lam_init + [1.0 - v for v in lam_init]], np.float32)
    dlam = np.concatenate([diff_lambda[l].reshape(1, 256) for l in ls], 1).astype(np.float32)
    return dict(wkv=np.ascontiguousarray(np.stack(wkv), dtype=np.float32), wqg=np.ascontiguousarray(np.stack(wqg), dtype=np.float32),
                wo=np.ascontiguousarray(np.stack(wo), dtype=np.float32), gains=g, lconst=lconst, dlam=np.ascontiguousarray(dlam))


def kernel(x, rel_bias, pre_norm_g, w_in, diff_lambda, diff_subln_g, q_norm_g, k_norm_g, w_out, post_norm_g):
    x = np.asarray(x, np.float32)
    args = [np.asarray(a, np.float32) for a in (pre_norm_g, w_in, diff_lambda, diff_subln_g, q_norm_g, k_norm_g, w_out, post_norm_g)]
    rel_bias = np.asarray(rel_bias, np.float32)
    stat = [_static_inputs(h, rel_bias) for h in range(2)]
    perms = [_local_perm(h) for h in range(2)]
    if FUSED:
        nc = _get_prog("fused", [0, 1], [8, 4], True)
        wi = _weight_inputs([0, 1], *args)
        in_maps = []
        for c in range(8):
            b, half = c // 2, c % 2
            m = dict(stat[half])
            m.update(wi)
            m["xT"] = np.ascontiguousarray(x[b][perms[half]].T)
            in_maps.append(m)
        res = run_bass_kernel_spmd(nc, in_maps, core_ids=list(range(8)))
        outp = np.empty_like(x)
        for c in range(8):
            b, half = c // 2, c % 2
            outp[b, half * SH:(half + 1) * SH, :] = res.results[c]["out"].T
        return outp
    cur = x
    nc = _get_prog("single", [0], [4], False)
    for l in range(L):
        wi = _weight_inputs([l], *args)
        in_maps = []
        for c in range(8):
            b, half = c // 2, c % 2
            m = dict(stat[half])
            m.update(wi)
            m["xT"] = np.ascontiguousarray(cur[b][perms[half]].T)
            in_maps.append(m)
        res = run_bass_kernel_spmd(nc, in_maps, core_ids=list(range(8)))
        nxt_x = np.empty_like(cur)
        for c in range(8):
            b, half = c // 2, c % 2
            nxt_x[b, half * SH:(half + 1) * SH, :] = res.results[c]["out"].T
        cur = nxt_x
    return cur
```
